# Optimizing a Trainium2 kernel written in Bass

```python
import math
import jax, jax.numpy as jnp
from jax import lax
import numpy as np

D_MODEL = 4096
BATCH = 4
SEQ = 2048
DEPTH = 2
DEC_BATCH = 8
DEC_SEQ = 4
PAST_LEN = 16384
PAGE_SIZE = 128

D_MIX = D_MODEL
D_ATT = D_MIX // 2
D_LRU = D_MIX - D_ATT
HEAD_DIM = 128
N_HEADS = D_ATT // HEAD_DIM
N_LRU_BLOCKS = 16
LRU_BLOCK = D_LRU // N_LRU_BLOCKS
CONV_W = 4
LRU_C = 8.0
DILATED_GROUPS = ((128, 1), (512, 4), (2048, 16))
WINDOW_MAX = 2048
N_BUCKETS = 32
MAX_EXACT = N_BUCKETS // 2
MAX_DISTANCE = WINDOW_MAX
EPS = 1e-6
ATT_SCALE = HEAD_DIM ** -0.5
D_IN = 4 * D_ATT + 2 * D_LRU

kernel_name = "hymba_rglru_dilated_swa_step"


def rmsnorm(x, g):
    xf = x.astype(jnp.float32)
    y = xf * lax.rsqrt(jnp.mean(xf * xf, axis=-1, keepdims=True) + EPS)
    return (y * g.astype(jnp.float32)).astype(x.dtype)


def rel_bucket(dist):
    d = dist.astype(jnp.float32)
    large = MAX_EXACT + jnp.log(jnp.maximum(d, 1.0) / MAX_EXACT) / math.log(MAX_DISTANCE / MAX_EXACT) * (N_BUCKETS - MAX_EXACT)
    large = jnp.minimum(large.astype(jnp.int32), N_BUCKETS - 1)
    return jnp.where(dist < MAX_EXACT, dist, large)


def modulate(x, c, g_pre, w_ada, b_ada):
    mod = jnp.einsum('bd,de->be', jax.nn.silu(c), w_ada) + b_ada
    shift, scale, gate = jnp.split(mod, 3, axis=-1)
    h = rmsnorm(x, g_pre) * (1 + scale[:, None]) + shift[:, None]
    return h, gate


def split_proj(h, w_in):
    p = jnp.einsum('btd,de->bte', h, w_in)
    return jnp.split(p, [D_ATT, 2 * D_ATT, 3 * D_ATT, 4 * D_ATT, 4 * D_ATT + D_LRU], axis=-1)


def to_heads(t):
    return t.reshape(t.shape[0], t.shape[1], N_HEADS, HEAD_DIM)


def dilated_prompt(q, k, v, rel_table, window, dil):
    B, S, H, Dh = q.shape
    L = window // dil
    m = -(-S // (dil * L))
    S_pad = m * dil * L

    def to_res(t):
        t = jnp.pad(t, ((0, 0), (0, S_pad - S), (0, 0), (0, 0))).reshape(B, m * L, dil, H, Dh)
        return jnp.moveaxis(t, 2, 1).reshape(B, dil, m, L, H, Dh)

    def with_prev(t):
        prev = jnp.pad(t, ((0, 0), (0, 0), (1, 0), (0, 0), (0, 0), (0, 0)))[:, :, :m]
        return jnp.concatenate([prev, t], axis=3)

    qr = to_res(q)
    kb = with_prev(to_res(k))
    vb = with_prev(to_res(v))
    qi = jnp.arange(L)[:, None]
    kj = jnp.arange(2 * L)[None, :]
    dist = qi + L - kj
    band = (dist >= 0) & (dist <= L)
    has_prev = jnp.arange(m)[:, None, None] > 0
    valid = band[None] & (has_prev | (kj[None] >= L))
    bias = rel_table[rel_bucket(jnp.clip(dist, 0, L) * dil)]
    bias = jnp.moveaxis(bias, 2, 0).astype(jnp.float32)
    logits = jnp.einsum('brmqhd,brmkhd->brmhqk', qr, kb, preferred_element_type=jnp.float32) * ATT_SCALE + bias
    logits = jnp.where(valid[:, None], logits, -jnp.inf)
    lse = jax.nn.logsumexp(logits, axis=-1)
    p = jnp.exp(logits - lse[..., None])
    o = jnp.einsum('brmhqk,brmkhd->brmqhd', p.astype(vb.dtype), vb, preferred_element_type=jnp.float32)

    def from_res(t):
        rest = t.shape[4:]
        t = t.reshape((B, dil, m * L) + rest)
        return jnp.moveaxis(t, 1, 2).reshape((B, S_pad) + rest)[:, :S]

    return from_res(o), from_res(jnp.moveaxis(lse, 4, 3))


def dilated_sample(q, k_all, v_all, rel_table, window, dil):
    T = q.shape[1]
    c_len = k_all.shape[1] - T
    L = window // dil
    offs = jnp.arange(L + 1) * dil
    idx = c_len + jnp.arange(T)[:, None] - offs[None, :]
    valid = idx >= 0
    idx = jnp.maximum(idx, 0)
    kg = k_all[:, idx]
    vg = v_all[:, idx]
    bias = rel_table[rel_bucket(offs)].T.astype(jnp.float32)
    logits = jnp.einsum('bthd,btkhd->bthk', q, kg, preferred_element_type=jnp.float32) * ATT_SCALE + bias
    logits = jnp.where(valid[:, None, :], logits, -jnp.inf)
    lse = jax.nn.logsumexp(logits, axis=-1)
    p = jnp.exp(logits - lse[..., None])
    o = jnp.einsum('bthk,btkhd->bthd', p.astype(vg.dtype), vg, preferred_element_type=jnp.float32)
    return o, lse


def mix_dilations(outs):
    o = jnp.stack([a for a, _ in outs])
    lse = jnp.stack([b for _, b in outs])
    w = jax.nn.softmax(lse, axis=0)
    return jnp.einsum('gbth,gbthd->bthd', w, o)


def lru_branch(x_ext, h0, w_conv, b_conv, w_a, b_a, w_x, b_x, lam):
    B = x_ext.shape[0]
    T = x_ext.shape[1] - (CONV_W - 1)
    xc = b_conv + x_ext[:, 0:T] * w_conv[0]
    for j in range(1, CONV_W):
        xc = xc + x_ext[:, j:j + T] * w_conv[j]
    xb = xc.reshape(B, T, N_LRU_BLOCKS, LRU_BLOCK)
    r = jax.nn.sigmoid((jnp.einsum('btnd,nde->btne', xb, w_a).reshape(B, T, D_LRU) + b_a).astype(jnp.float32))
    i = jax.nn.sigmoid((jnp.einsum('btnd,nde->btne', xb, w_x).reshape(B, T, D_LRU) + b_x).astype(jnp.float32))
    log_a = -LRU_C * r * jax.nn.softplus(-lam.astype(jnp.float32))
    a = jnp.exp(log_a)
    b = jnp.sqrt(-jnp.expm1(2.0 * log_a)) * (i * xc.astype(jnp.float32))

    def step(h, ab):
        a_t, b_t = ab
        h = a_t * h + b_t
        return h, h

    h_last, hs = lax.scan(step, h0.astype(jnp.float32), (jnp.swapaxes(a, 0, 1), jnp.swapaxes(b, 0, 1)))
    return jnp.swapaxes(hs, 0, 1).astype(x_ext.dtype), h_last


def finish(x, gate, o_att, g_att, y_lru, g_lru, w_out, g_post):
    B, T = x.shape[0], x.shape[1]
    u = jnp.concatenate([o_att.reshape(B, T, D_ATT).astype(x.dtype) * jax.nn.silu(g_att),
                         y_lru * jax.nn.silu(g_lru)], axis=-1)
    y = rmsnorm(jnp.einsum('btm,md->btd', u, w_out), g_post)
    return x + gate[:, None] * y


def setup_inputs(seed: int = 0) -> dict:
    key = jax.random.key(seed)
    ks = jax.random.split(key, 22)
    f32 = jnp.float32
    c_len = min(WINDOW_MAX, PAST_LEN)

    def nrm(k, shape, s=1.0):
        return s * jax.random.normal(k, shape, f32)

    u = jax.random.uniform(ks[19], (DEPTH, D_LRU), f32, 0.9, 0.999)
    a0 = u ** (1.0 / LRU_C)
    return {
        "x_prompt": nrm(ks[0], (BATCH, SEQ, D_MODEL)),
        "x_sample": nrm(ks[1], (DEC_BATCH, DEC_SEQ, D_MODEL)),
        "cache_k": nrm(ks[2], (DEPTH, DEC_BATCH, c_len, N_HEADS, HEAD_DIM)),
        "cache_v": nrm(ks[3], (DEPTH, DEC_BATCH, c_len, N_HEADS, HEAD_DIM)),
        "state_h": nrm(ks[4], (DEPTH, DEC_BATCH, D_LRU), 0.5),
        "state_conv": nrm(ks[5], (DEPTH, DEC_BATCH, CONV_W - 1, D_LRU)),
        "c_prompt": nrm(ks[6], (BATCH, D_MODEL)),
        "c_sample": nrm(ks[7], (DEC_BATCH, D_MODEL)),
        "rel_table": nrm(ks[8], (N_BUCKETS, N_HEADS), 0.5),
        "w_ada": nrm(ks[9], (DEPTH, D_MODEL, 3 * D_MODEL), D_MODEL ** -0.5),
        "b_ada": nrm(ks[10], (DEPTH, 3 * D_MODEL), 0.01),
        "g_pre": 1.0 + nrm(ks[11], (DEPTH, D_MODEL), 0.05),
        "w_in": nrm(ks[12], (DEPTH, D_MODEL, D_IN), D_MODEL ** -0.5),
        "w_conv": nrm(ks[13], (DEPTH, CONV_W, D_LRU), CONV_W ** -0.5),
        "b_conv": nrm(ks[14], (DEPTH, D_LRU), 0.01),
        "w_a": nrm(ks[15], (DEPTH, N_LRU_BLOCKS, LRU_BLOCK, LRU_BLOCK), LRU_BLOCK ** -0.5),
        "b_a": nrm(ks[16], (DEPTH, D_LRU), 0.01),
        "w_x": nrm(ks[17], (DEPTH, N_LRU_BLOCKS, LRU_BLOCK, LRU_BLOCK), LRU_BLOCK ** -0.5),
        "b_x": nrm(ks[18], (DEPTH, D_LRU), 0.01),
        "lam": jnp.log(a0) - jnp.log1p(-a0),
        "w_out": nrm(ks[20], (DEPTH, D_MIX, D_MODEL), D_MIX ** -0.5),
        "g_post": 1.0 + nrm(ks[21], (DEPTH, D_MODEL), 0.05),
    }


def reference(x_prompt, x_sample, cache_k, cache_v, state_h, state_conv, c_prompt, c_sample,
              rel_table, w_ada, b_ada, g_pre, w_in, w_conv, b_conv, w_a, b_a, w_x, b_x, lam, w_out, g_post):
    xp, xs = x_prompt, x_sample
    keep = min(WINDOW_MAX, xp.shape[1])
    kp_l, vp_l, hp_l, cp_l = [], [], [], []
    ks_l, vs_l, hs_l, cs_l = [], [], [], []
    for l in range(DEPTH):
        lru_w = (w_conv[l], b_conv[l], w_a[l], b_a[l], w_x[l], b_x[l], lam[l])
        hn, gate = modulate(xp, c_prompt, g_pre[l], w_ada[l], b_ada[l])
        q, k, v, g_att, x_lru, g_lru = split_proj(hn, w_in[l])
        qh, kh, vh = to_heads(q), to_heads(k), to_heads(v)
        o_att = mix_dilations([dilated_prompt(qh, kh, vh, rel_table, w, d) for (w, d) in DILATED_GROUPS])
        x_ext = jnp.pad(x_lru, ((0, 0), (CONV_W - 1, 0), (0, 0)))
        h0 = jnp.zeros((xp.shape[0], D_LRU), jnp.float32)
        y_lru, h_last = lru_branch(x_ext, h0, *lru_w)
        xp = finish(xp, gate, o_att, g_att, y_lru, g_lru, w_out[l], g_post[l])
        kp_l.append(kh[:, -keep:])
        vp_l.append(vh[:, -keep:])
        hp_l.append(h_last.astype(x_lru.dtype))
        cp_l.append(x_ext[:, -(CONV_W - 1):])
        hn, gate = modulate(xs, c_sample, g_pre[l], w_ada[l], b_ada[l])
        q, k, v, g_att, x_lru, g_lru = split_proj(hn, w_in[l])
        qh, kh, vh = to_heads(q), to_heads(k), to_heads(v)
        k_all = jnp.concatenate([cache_k[l].astype(kh.dtype), kh], axis=1)
        v_all = jnp.concatenate([cache_v[l].astype(vh.dtype), vh], axis=1)
        o_att = mix_dilations([dilated_sample(qh, k_all, v_all, rel_table, w, d) for (w, d) in DILATED_GROUPS])
        x_ext = jnp.concatenate([state_conv[l].astype(x_lru.dtype), x_lru], axis=1)
        y_lru, h_last = lru_branch(x_ext, state_h[l], *lru_w)
        xs = finish(xs, gate, o_att, g_att, y_lru, g_lru, w_out[l], g_post[l])
        ks_l.append(kh)
        vs_l.append(vh)
        hs_l.append(h_last.astype(x_lru.dtype))
        cs_l.append(x_ext[:, -(CONV_W - 1):])
    return (xp, xs, jnp.stack(kp_l), jnp.stack(vp_l), jnp.stack(hp_l), jnp.stack(cp_l),
            jnp.stack(ks_l), jnp.stack(vs_l), jnp.stack(hs_l), jnp.stack(cs_l))
```

```python
import contextlib
import math
import numpy as np
import concourse.bass as bass
import concourse.mybir as mybir
from concourse.bass_utils import run_bass_kernel_spmd

F32, BF16 = mybir.dt.float32, mybir.dt.bfloat16
AF = mybir.ActivationFunctionType
ALU = mybir.AluOpType

D = 4096
NP = 2048
NS = 8
NT = NP + NS
H = 16
NB = 16
EPS = 1e-6
SCALE = 128 ** -0.5
TW = 2560
TVL = 2688
TOFF = 511
DEPTH = 2
NGI = 48
NGO = 16


class Sem:
    pass


class Trk:
    def __init__(self, nc, es):
        self.nc, self.es = nc, es
        self.eng = {"pe": nc.tensor, "act": nc.scalar, "dve": nc.vector, "pool": nc.gpsimd, "sp": nc.sync}
        self.selfsem = {}
        self.allsems = []
        self.W, self.R = {}, {}
        self.seen = {k: {} for k in self.eng}
        self.dmasem = {}
        self.cnt = 0
        self.epoch()

    def newsem(self, name):
        s = Sem()
        self.cnt += 1
        s.h = self.es.enter_context(self.nc.semaphore(f"{name}{self.cnt}"))
        s.n = 0
        self.allsems.append(s)
        return s

    def epoch(self):
        for k in self.eng:
            self.selfsem[k] = self.newsem("e" + k)

    def _wait(self, e, sem, val):
        if val <= 0 or self.seen[e].get(sem, 0) >= val:
            return
        self.eng[e].wait_ge(sem.h, val)
        self.seen[e][sem] = val

    def op(self, e, fn, r=(), w=(), pw=(), aw=(), sem=None, k=1):
        waits = {}

        def add(d):
            for s, v in d.items():
                if waits.get(s, 0) < v:
                    waits[s] = v
        for b in r:
            add(self.W.get(b, {}))
            if isinstance(b, tuple) and b[0] == "PS":
                for s_, v_ in self.R.get(b, {}).items():
                    if s_ is not self.selfsem[e] and waits.get(s_, 0) < v_:
                        waits[s_] = v_
        for b in list(w) + list(aw):
            add(self.W.get(b, {}))
            add(self.R.get(b, {}))
        for s, v in waits.items():
            self._wait(e, s, v)
        ins = fn(self.eng[e])
        if sem is None:
            sem = self.selfsem[e]
        ins.then_inc(sem.h, k)
        sem.n += k
        for b in w:
            self.W[b] = {sem: sem.n}
            self.R[b] = {}
        for b in list(pw) + list(aw):
            self.W.setdefault(b, {})[sem] = sem.n
        for b in r:
            self.R.setdefault(b, {})[sem] = sem.n
        return (sem, sem.n)

    def dma(self, q, out, in_, r=(), w=(), pw=(), aw=(), key=None, **kw):
        if key is None:
            key = w[0] if w else (aw[0] if aw else (pw[0] if pw else r[0]))
        ds = self.dmasem.get(key)
        if ds is None:
            ds = self.dmasem[key] = self.newsem("d")
        return self.op(q, lambda E: E.dma_start(out=out, in_=in_, **kw), r=r, w=w, pw=pw, aw=aw, sem=ds, k=16)

    def barrier(self):
        for e in self.eng:
            for s in self.allsems:
                self._wait(e, s, s.n)
        self.W.clear()
        self.R.clear()


def build_nc(stop=99):
    nc = bass.Bass("TRN2", target_bir_lowering=False)

    def din(name, shape, dt=F32):
        return nc.dram_tensor(name, shape, dt, kind="ExternalInput")

    def dout(name, shape, dt=F32):
        return nc.dram_tensor(name, shape, dt, kind="ExternalOutput")

    def dscr(name, shape, dt=F32):
        return nc.dram_tensor(name, shape, dt, kind="Internal")

    x_in = din("x_in", [NT, D]).ap()
    c3 = din("c3", [3, D]).ap()
    ck = din("ck", [DEPTH, 2, NP, H * 128]).ap()
    cv = din("cv", [DEPTH, 2, NP, H * 128]).ap()
    sh = din("sh", [DEPTH, 2, 2048]).ap()
    sconvT = din("sconvT", [DEPTH, 2, 2048, 3]).ap()
    rel = din("rel", [32, 16]).ap()
    cpad = din("cpad", [32, TVL]).ap()
    jmat_d = din("jmat", [128, 128]).ap()
    ident_d = din("ident", [128, 128]).ap()
    wada = din("wada", [DEPTH, NGI, 128, 8192]).ap()
    bada_h = din("bada", [DEPTH, 12288])
    gpreT = din("gpreT", [DEPTH, 128, 32]).ap()
    win = din("win", [DEPTH, NGI, 128, 8192]).ap()
    wout = din("wout", [DEPTH, NGO, 128, 8192]).ap()
    wconvT = din("wconvT", [DEPTH, 128, NB, 4]).ap()
    lrup = din("lrup", [DEPTH, 4, 128, NB]).ap()
    wa_t = din("wa_t", [DEPTH, NB, 128, 128]).ap()
    wx_t = din("wx_t", [DEPTH, NB, 128, 128]).ap()
    gpost_h = din("gpost", [DEPTH, D])

    yp = dout("yp", [NT, D]).ap()
    kp = dout("kp", [DEPTH, NT, H * 128]).ap()
    vp = dout("vp", [DEPTH, NT, H * 128]).ap()
    hp = dout("hp", [DEPTH, 3, 2048]).ap()
    convT = dout("convT", [DEPTH, 3, 2048, 3]).ap()

    qT_scr = dscr("qT_scr", [H, 128, NT], BF16).ap()
    kT_scr = dscr("kT_scr", [H, 128, NT], BF16).ap()
    vb_scr = dscr("vb_scr", [NT, H * 128], BF16).ap()
    sg_scr = dscr("sg_scr", [H, 128, NT]).ap()
    xl_scr = dscr("xl_scr", [NB, 128, NT]).ap()
    sgl_scr = dscr("sgl_scr", [NB, 128, NT]).ap()
    y_scr = dscr("y_scr", [NT, D]).ap()
    x1_scr = dscr("x1_scr", [NT, D]).ap()
    mod_h = dscr("mod_scr", [DEPTH, 3, D])
    mod_scr = mod_h.ap()
    tv_h = dscr("tv_scr", [H, TVL])
    tv_scr = tv_h.ap()
    wb_scr = dscr("wb_scr", [H, 128, TW], BF16).ap()

    uid = [0]

    def sb(scope, shape, dt, name="t"):
        uid[0] += 1
        return scope.enter_context(nc.sbuf_tensor(f"{name}{uid[0]}", shape, dt))

    with contextlib.ExitStack() as es:
        t = Trk(nc, es)
        PS = [es.enter_context(nc.psum_tensor(f"ps{i}", [128, 512], F32)) for i in range(8)]
        ident = sb(es, [128, 128], F32, "ident")
        jmat = sb(es, [128, 128], F32, "jmat")
        ones = sb(es, [128, 128], BF16, "ones")
        AB = sb(es, [128, DEPTH * 2 * 32 * 3], F32, "AB")
        scT = sb(es, [128, 96], BF16, "scT")

        def ABap(l, ab, kt, g):
            o = ((l * 2 + ab) * 32 + kt) * 3 + g
            return AB[:, o:o + 1]

        t.dma("sp", ident[:], ident_d, w=["ident"])
        t.dma("sp", jmat[:], jmat_d, w=["jmat"])
        t.op("dve", lambda E: E.memset(ones[:], 1.0), w=["ones"])

        with contextlib.ExitStack() as sc:
            rel_sb = sb(sc, [32, 16], F32)
            cp_sb = sb(sc, [32, TVL], F32)
            et = sb(sc, [32, 16], F32)
            tv_sb = sb(sc, [16, TVL], F32)
            hk = [sb(sc, [128, TW], F32) for _ in range(2)]
            wbm = [sb(sc, [128, TW], BF16) for _ in range(2)]
            t.dma("sp", rel_sb[:], rel, w=["rel"])
            t.dma("sp", cp_sb[:], cpad, w=["cp"])
            t.op("act", lambda E: E.activation(out=et[:], in_=rel_sb[:], func=AF.Exp), r=["rel"], w=["et"])
            for ch in range(6):
                n = min(512, TVL - ch * 512)
                t.op("pe", lambda E, ch=ch, n=n: E.matmul(PS[ch][0:16, 0:n], lhsT=et[:], rhs=cp_sb[:, ch * 512:ch * 512 + n],
                                                           start=True, stop=True), r=["et", "cp"], w=[("PS", ch)])
                t.op("dve", lambda E, ch=ch, n=n: E.tensor_copy(out=tv_sb[:, ch * 512:ch * 512 + n], in_=PS[ch][0:16, 0:n]),
                     r=[("PS", ch)], pw=["tv"])
            t.dma("sp", tv_scr, tv_sb[:], r=["tv"], w=["tv_scr"])
            bk = 0
            for h in range(H):
                s = h % 2
                src = bass.AP(tv_h, h * TVL, [[1, 128], [1, TW]])
                t.dma("sp", hk[s][:], src, r=["tv_scr"], w=[("hk", s)])
                for ch in range(5):
                    b = bk % 8
                    bk += 1
                    t.op("pe", lambda E, s=s, ch=ch, b=b: E.matmul(PS[b][:, :], lhsT=jmat[:], rhs=hk[s][:, ch * 512:(ch + 1) * 512],
                                                                    start=True, stop=True), r=["jmat", ("hk", s)], w=[("PS", b)])
                    first = (ch == 0)
                    if ch % 2 == 0:
                        t.op("act", lambda E, s=s, ch=ch, b=b: E.activation(out=wbm[s][:, ch * 512:(ch + 1) * 512], in_=PS[b][:, :], func=AF.Copy),
                             r=[("PS", b)], w=[("wbm", s)] if first else [], aw=[] if first else [("wbm", s)])
                    else:
                        t.op("dve", lambda E, s=s, ch=ch, b=b: E.tensor_copy(out=wbm[s][:, ch * 512:(ch + 1) * 512], in_=PS[b][:, :]),
                             r=[("PS", b)], aw=[("wbm", s)])
                t.dma("sp", wb_scr[h], wbm[s][:], r=[("wbm", s)], pw=["wb_scr"])
        t.barrier()
        if stop <= 0:
            return nc

        with contextlib.ExitStack() as sc:
            c3s = sb(sc, [3, D], F32)
            Wb = [sb(sc, [128, 8192], BF16) for _ in range(2)]
            bb = [sb(sc, [3, 256], F32) for _ in range(2)]
            modg = [sb(sc, [3, 256], F32) for _ in range(2)]
            modT = sb(sc, [128, 192], F32)
            gpr = sb(sc, [128, 32], F32)
            tmp1 = sb(sc, [128, 96], F32)
            t.dma("sp", c3s[:], c3, w=["c3s"])
            t.op("act", lambda E: E.activation(out=c3s[:], in_=c3s[:], func=AF.Silu), w=["c3s"])
            for kt in range(32):
                t.op("pe", lambda E, kt=kt: E.transpose(out=PS[0][:, kt * 3:(kt + 1) * 3], in_=c3s[:, kt * 128:(kt + 1) * 128],
                                                        identity=ident[0:3, 0:3]), r=["c3s", "ident"],
                     w=[("PS", 0)] if kt == 0 else [], pw=[] if kt == 0 else [("PS", 0)])
            t.op("dve", lambda E: E.tensor_copy(out=scT[:], in_=PS[0][:, 0:96]), r=[("PS", 0)], w=["scT"])
            gi = 0
            for l in range(DEPTH):
                for g in range(NGI):
                    s = gi % 2
                    pb = 1 + (gi % 4)
                    gi += 1
                    t.dma("pool", Wb[s][:], wada[l, g], w=[("W", s)], max_dma_last_dim=8192)
                    t.dma("sp", bb[s][:], bass.AP(bada_h, l * 12288 + g * 256, [[0, 3], [1, 256]]), w=[("bb", s)])

                    def mm(E, s=s, pb=pb):
                        ins = None
                        for kt in range(32):
                            ins = E.matmul(PS[pb][0:3, 0:256], lhsT=scT[:, kt * 3:(kt + 1) * 3], rhs=Wb[s][:, kt * 256:(kt + 1) * 256],
                                           start=(kt == 0), stop=(kt == 31))
                        return ins
                    t.op("pe", mm, r=[("W", s), "scT"], w=[("PS", pb)])
                    t.op("dve", lambda E, s=s, pb=pb: E.tensor_tensor(out=modg[s][:], in0=PS[pb][0:3, 0:256], in1=bb[s][:], op=ALU.add),
                         r=[("PS", pb), ("bb", s)], w=[("modg", s)])
                    if g < 32:
                        for cb in range(2):
                            o = (g * 2 + cb) * 3
                            fst = (g == 0 and cb == 0)
                            t.op("pe", lambda E, s=s, cb=cb, o=o: E.transpose(out=PS[7][:, o:o + 3], in_=modg[s][:, cb * 128:(cb + 1) * 128],
                                                                              identity=ident[0:3, 0:3]),
                                 r=[("modg", s), "ident"], w=[("PS", 7)] if fst else [], pw=[] if fst else [("PS", 7)])
                    else:
                        t.dma("sp", mod_scr[l, :, (g - 32) * 256:(g - 31) * 256], modg[s][:], r=[("modg", s)], pw=["mod_scr"])
                t.op("dve", lambda E: E.tensor_copy(out=modT[:], in_=PS[7][:, 0:192]), r=[("PS", 7)], w=["modT"])
                t.dma("sp", gpr[:], gpreT[l], w=["gpr"])
                t.op("dve", lambda E: E.tensor_scalar(out=tmp1[:], in0=modT[:, 96:192], scalar1=1.0, scalar2=None, op0=ALU.add),
                     r=["modT"], w=["tmp1"])
                for g3 in range(3):
                    oa = (l * 2 + 0) * 96
                    ob = (l * 2 + 1) * 96
                    t.op("dve", lambda E, g3=g3, oa=oa: E.tensor_tensor(out=AB[:, oa:oa + 96].rearrange("p (k g) -> p k g", g=3)[:, :, g3], in0=tmp1[:].rearrange("p (k g) -> p k g", g=3)[:, :, g3], in1=gpr[:], op=ALU.mult),
                         r=["tmp1", "gpr"], pw=["AB"])
                t.op("dve", lambda E, ob=ob: E.tensor_copy(out=AB[:, ob:ob + 96], in_=modT[:, 0:96]), r=["modT"], pw=["AB"])
        t.barrier()

        def phase_a(l, xsrc, BIG):
            with contextlib.ExitStack() as sc:
                xa = [sb(sc, [128, D], F32) for _ in range(2)]
                junk = sb(sc, [128, D], BF16)
                stat = sb(sc, [128, 17 * 4], F32)
                diag = [sb(sc, [128, 128], F32) for _ in range(2)]
                ei = 0
                bi = 0
                for tt in range(_DBG['ntt']):
                    M = 128 if tt < 16 else NS
                    r0 = tt * 128
                    s = tt % 2
                    so = tt * 4
                    t.dma("sp", xa[s][0:M, :], xsrc[r0:r0 + M, :], w=[("xa", s)])
                    t.op("act", lambda E, s=s, M=M, so=so: E.activation(out=junk[0:M, :], in_=xa[s][0:M, :], func=AF.Square,
                                                                         accum_out=stat[0:M, so:so + 1]),
                         r=[("xa", s)], w=["junk", ("st", 0)])
                    t.op("dve", lambda E, M=M, so=so: E.tensor_scalar(out=stat[0:M, so + 1:so + 2], in0=stat[0:M, so:so + 1], scalar1=1.0 / D,
                                                                       scalar2=EPS, op0=ALU.mult, op1=ALU.add), r=[("st", 0)], w=[("st", 1)])
                    t.op("act", lambda E, M=M, so=so: E.activation(out=stat[0:M, so + 2:so + 3], in_=stat[0:M, so + 1:so + 2], func=AF.Sqrt),
                         r=[("st", 1)], w=[("st", 2)])
                    t.op("dve", lambda E, M=M, so=so: E.reciprocal(out=stat[0:M, so + 3:so + 4], in_=stat[0:M, so + 2:so + 3]),
                         r=[("st", 2)], w=[("st", 3)])
                    t.op("dve", lambda E, s=s, M=M, so=so: E.tensor_scalar(out=diag[s][0:M, 0:M], in0=ident[0:M, 0:M], scalar1=stat[0:M, so + 3:so + 4],
                                                                            scalar2=None, op0=ALU.mult), r=[("st", 3), "ident"], w=[("dg", s)])
                    for q4 in range(8):
                        b = bi % 6
                        bi += 1

                        def mm(E, s=s, M=M, q4=q4, b=b):
                            ins = None
                            for j in range(4):
                                kt = q4 * 4 + j
                                ins = E.matmul(PS[b][:, j * 128:j * 128 + M], lhsT=xa[s][0:M, kt * 128:(kt + 1) * 128], rhs=diag[s][0:M, 0:M],
                                               start=True, stop=True)
                            return ins
                        t.op("pe", mm, r=[("xa", s), ("dg", s)], w=[("PS", b)])
                        for j in range(4):
                            kt = q4 * 4 + j
                            grps = [(0, M, 0)] if tt < 16 else [(0, 4, 1), (4, 8, 2)]
                            for (c0, c1, g3) in grps:
                                ei += 1
                                dst = BIG[:, kt * NT + r0 + c0:kt * NT + r0 + c1]
                                srcp = PS[b][:, j * 128 + c0:j * 128 + c1]
                                if b % 2 == 0:
                                    t.op("act", lambda E, dst=dst, srcp=srcp, kt=kt, g3=g3: E.activation(
                                        out=dst, in_=srcp, func=AF.Identity, scale=ABap(l, 0, kt, g3), bias=ABap(l, 1, kt, g3)),
                                        r=[("PS", b), "AB"], pw=["BIG"])
                                else:
                                    t.op("dve", lambda E, dst=dst, srcp=srcp, kt=kt, g3=g3: E.tensor_scalar(
                                        out=dst, in0=srcp, scalar1=ABap(l, 0, kt, g3), scalar2=ABap(l, 1, kt, g3), op0=ALU.mult, op1=ALU.add),
                                        r=[("PS", b), "AB"], pw=["BIG"])

        def proj(l, wsrc, ng, kind_of, BIG):
            with contextlib.ExitStack() as sc:
                Wb = [sb(sc, [128, 8192], BF16) for _ in range(2)]
                stF = [sb(sc, [128, 1032], F32) for _ in range(2)]
                stH = [sb(sc, [128, 1032], BF16) for _ in range(2)]
                sT = [sb(sc, [128, 512], F32) for _ in range(2)]
                sTb = [sb(sc, [128, 512], BF16) for _ in range(2)]
                setbanks = [(0, 1, 2), (3, 4, 5)]
                st = {"si": 0, "tb": 0}

                def epi_pe(info):
                    kind, cbi, h2, s = info
                    if kind not in ("k", "v", "y"):
                        return
                    tok0 = h2 * 1024
                    tiles = [(j, 128) for j in range(8)] + ([(8, NS)] if h2 == 1 else [])
                    for jb in range(0, len(tiles), 4):
                        batch = tiles[jb:jb + 4]
                        tbk = st["tb"] % 2
                        st["tb"] += 1
                        bank = 6 + tbk

                        def tr(E, batch=batch, s=s, bank=bank):
                            ins = None
                            for jj, (j, M) in enumerate(batch):
                                ins = E.transpose(out=PS[bank][0:M, jj * 128:(jj + 1) * 128], in_=stF[s][:, j * 128:j * 128 + M], identity=ident[:, :])
                            return ins
                        t.op("pe", tr, r=[("stF", s), "ident"], w=[("PS", bank)])
                        M = batch[0][1]
                        nj = len(batch)
                        t.op("dve", lambda E, tbk=tbk, bank=bank, M=M, nj=nj: E.tensor_copy(out=sT[tbk][0:M, 0:nj * 128], in_=PS[bank][0:M, 0:nj * 128]),
                             r=[("PS", bank)], w=[("sT", tbk)])
                        r0 = tok0 + batch[0][0] * 128
                        nrow = (nj - 1) * 128 + M
                        if kind == "y":
                            dst = y_scr[r0:r0 + nrow, cbi * 128:(cbi + 1) * 128]
                        elif kind == "k":
                            dst = kp[l, r0:r0 + nrow, (cbi - 16) * 128:(cbi - 15) * 128]
                        else:
                            dst = vp[l, r0:r0 + nrow, (cbi - 32) * 128:(cbi - 31) * 128]
                        if M == 128:
                            dst = dst.rearrange("(j p) c -> p j c", p=128)
                            srcv = sT[tbk][:, 0:nj * 128].rearrange("p (j c) -> p j c", c=128)
                        else:
                            srcv = sT[tbk][0:M, 0:128]
                        t.dma("sp", dst, srcv, r=[("sT", tbk)], pw=["outkvy"], key=("sT", tbk))
                        if kind == "v":
                            t.op("pool", lambda E, tbk=tbk, M=M, nj=nj: E.tensor_copy(out=sTb[tbk][0:M, 0:nj * 128], in_=sT[tbk][0:M, 0:nj * 128]),
                                 r=[("sT", tbk)], w=[("sTb", tbk)])
                            dstb = vb_scr[r0:r0 + nrow, (cbi - 32) * 128:(cbi - 31) * 128]
                            if M == 128:
                                dstb = dstb.rearrange("(j p) c -> p j c", p=128)
                                srcb = sTb[tbk][:, 0:nj * 128].rearrange("p (j c) -> p j c", c=128)
                            else:
                                srcb = sTb[tbk][0:M, 0:128]
                            t.dma("sp", dstb, srcb, r=[("sTb", tbk)], pw=["vb_scr"], key=("sTb", tbk))

                prev = None
                for g in range(ng):
                    ws = g % 2
                    t.dma("pool", Wb[ws][:], wsrc[g], w=[("W", ws)], max_dma_last_dim=8192)
                    for cb in range(2):
                        cbi = g * 2 + cb
                        kind = kind_of(cbi)
                        for h2 in range(2):
                            si = st["si"]
                            st["si"] += 1
                            s = si % 2
                            bs = setbanks[s]
                            tok0 = h2 * 1024
                            chunks = [(0, 512), (512, 512)] + ([(1024, NS)] if h2 == 1 else [])

                            def mm(E, ws=ws, cb=cb, tok0=tok0, chunks=chunks, bs=bs):
                                ins = None
                                for kt in range(32):
                                    for j, (c0, n) in enumerate(chunks):
                                        ins = E.matmul(PS[bs[j]][:, 0:n], lhsT=Wb[ws][:, kt * 256 + cb * 128:kt * 256 + (cb + 1) * 128],
                                                       rhs=BIG[:, kt * NT + tok0 + c0:kt * NT + tok0 + c0 + n], start=(kt == 0), stop=(kt == 31))
                                return ins
                            t.op("pe", mm, r=[("W", ws), "BIG"], w=[("PS", bs[j]) for j in range(len(chunks))])
                            tgt, tkey = (stH[s], ("stH", s)) if kind == "q" else (stF[s], ("stF", s))
                            fn = AF.Silu if kind in ("ga", "gl") else AF.Copy
                            for j, (c0, n) in enumerate(chunks):
                                t.op("act", lambda E, tgt=tgt, c0=c0, n=n, bj=bs[j], fn=fn: E.activation(out=tgt[:, c0:c0 + n], in_=PS[bj][:, 0:n], func=fn),
                                     r=[("PS", bs[j])], w=[tkey] if j == 0 else [], pw=[] if j == 0 else [tkey])
                            ntok = chunks[-1][0] + chunks[-1][1]
                            hh = cbi % 16
                            if kind == "q":
                                t.dma("sp", qT_scr[hh, :, tok0:tok0 + ntok], stH[s][:, 0:ntok], r=[tkey], pw=["qT_scr"], key=tkey)
                            elif kind == "k":
                                t.op("dve", lambda E, s=s, ntok=ntok: E.tensor_copy(out=stH[s][:, 0:ntok], in_=stF[s][:, 0:ntok]),
                                     r=[("stF", s)], w=[("stH", s)])
                                t.dma("sp", kT_scr[hh, :, tok0:tok0 + ntok], stH[s][:, 0:ntok], r=[("stH", s)], pw=["kT_scr"], key=("stH", s))
                            elif kind in ("ga", "xl", "gl"):
                                dsc = {"ga": sg_scr, "xl": xl_scr, "gl": sgl_scr}[kind]
                                t.dma("sp", dsc[hh, :, tok0:tok0 + ntok], stF[s][:, 0:ntok], r=[tkey], pw=["scr_" + kind], key=tkey)
                            if prev is not None:
                                epi_pe(prev)
                            prev = (kind, cbi, h2, s)
                epi_pe(prev)

        def kind_in(cbi):
            return ["q", "k", "v", "ga", "xl", "gl"][cbi // 16]

        def phase_c(l, BIG):
            with contextlib.ExitStack() as sc:
                qT = [sb(sc, [128, NT], BF16) for _ in range(2)]
                kT = [sb(sc, [128, NT], BF16) for _ in range(2)]
                V = [sb(sc, [128, 2048], BF16) for _ in range(2)]
                vn = [[sb(sc, [4, 128], BF16) for _ in range(2)] for _ in range(2)]
                wbm = [sb(sc, [128, TW], BF16) for _ in range(2)]
                sgc = [sb(sc, [128, 512], F32) for _ in range(2)]
                sgs = [sb(sc, [128, NS], F32) for _ in range(2)]
                Eb = [sb(sc, [128, 512], BF16) for _ in range(2)]
                Pb = [sb(sc, [128, 512], BF16) for _ in range(2)]
                rc = [sb(sc, [128, 512], F32) for _ in range(2)]
                tmpo = [sb(sc, [128, 512], F32) for _ in range(2)]
                Kc = sb(sc, [128, 2048], F32)
                KcT = sb(sc, [128, 2048], BF16)
                Vc = sb(sc, [128, 2048], BF16)
                Es = sb(sc, [128, 68], F32)
                Ps = sb(sc, [128, 68], BF16)
                rcs = sb(sc, [128, 4], F32)
                tms = sb(sc, [128, 4], F32)
                qi = 0
                ii = 0
                for h in range(H):
                    hs = h % 2
                    hc = slice(h * 128, (h + 1) * 128)
                    t.dma("sp", qT[hs][:], qT_scr[h], w=[("qT", hs)])
                    t.dma("sp", kT[hs][:], kT_scr[h], w=[("kT", hs)])
                    t.dma("sp", V[hs][:].rearrange("p (j c) -> p j c", c=128), vb_scr[0:NP, hc].rearrange("(j p) c -> p j c", p=128), w=[("V", hs)])
                    for db in range(2):
                        t.dma("sp", vn[hs][db][:], vb_scr[NP + db * 4:NP + db * 4 + 4, hc], w=[("vn", hs, db)])
                    t.dma("sp", wbm[hs][:], wb_scr[h], w=[("wbm", hs)])
                    t.dma("sp", sgs[hs][:], sg_scr[h, :, NP:NT], w=[("sgs", hs)])
                    for qc in range(4):
                        qs = qi % 2
                        qi += 1
                        t.dma("sp", sgc[qs][:], sg_scr[h, :, qc * 512:(qc + 1) * 512], w=[("sgc", qs)])
                        nkb = 4 * qc + 4
                        po, pd = 2 + qs, 4 + qs

                        def pv(i, kb, iis, nkb=nkb, po=po, pd=pd, hs=hs):
                            first, last = (i == 0), (i == nkb - 1)
                            t.op("pe", lambda E: E.matmul(PS[po][:, :], lhsT=V[hs][:, kb * 128:(kb + 1) * 128], rhs=Pb[iis][:, :], start=first, stop=last),
                                 r=[("V", hs), ("Pb", iis)], w=[("PS", po)] if first else [], pw=[] if first else [("PS", po)])
                            t.op("pe", lambda E: E.matmul(PS[pd][:, :], lhsT=ones[:, :], rhs=Pb[iis][:, :], start=first, stop=last),
                                 r=["ones", ("Pb", iis)], w=[("PS", pd)] if first else [], pw=[] if first else [("PS", pd)])
                        pend = None
                        for i in range(nkb):
                            kb = i
                            iis = ii % 2
                            ii += 1
                            off = 384 + qc * 512 - kb * 128
                            t.op("pe", lambda E, iis=iis, kb=kb: E.matmul(PS[iis][:, :], lhsT=kT[hs][:, kb * 128:(kb + 1) * 128],
                                                                          rhs=qT[hs][:, qc * 512:(qc + 1) * 512], start=True, stop=True),
                                 r=[("kT", hs), ("qT", hs)], w=[("PS", iis)])
                            t.op("act", lambda E, iis=iis: E.activation(out=Eb[iis][:, :], in_=PS[iis][:, :], func=AF.Exp, scale=SCALE),
                                 r=[("PS", iis)], w=[("Eb", iis)])
                            t.op("dve", lambda E, iis=iis, off=off: E.tensor_tensor(out=Pb[iis][:, :], in0=Eb[iis][:, :], in1=wbm[hs][:, off:off + 512], op=ALU.mult),
                                 r=[("Eb", iis), ("wbm", hs)], w=[("Pb", iis)])
                            if pend is not None:
                                pv(*pend)
                            pend = (i, kb, iis)
                        pv(*pend)
                        t.op("dve", lambda E, qs=qs, pd=pd: E.reciprocal(out=rc[qs][:, :], in_=PS[pd][:, :]), r=[("PS", pd)], w=[("rc", qs)])
                        t.op("dve", lambda E, qs=qs, po=po: E.tensor_tensor(out=tmpo[qs][:, :], in0=PS[po][:, :], in1=rc[qs][:, :], op=ALU.mult),
                             r=[("PS", po), ("rc", qs)], w=[("tmpo", qs)])
                        t.op("pool", lambda E, qs=qs, qc=qc: E.tensor_tensor(out=BIG[:, h * NT + qc * 512:h * NT + (qc + 1) * 512], in0=tmpo[qs][:, :],
                                                                             in1=sgc[qs][:, :], op=ALU.mult),
                             r=[("tmpo", qs), ("sgc", qs)], pw=["BIG"])
                    for db in range(2):
                        t.dma("sp", Kc[:].rearrange("p (j c) -> p j c", c=128), ck[l, db, :, hc].rearrange("(j p) c -> p j c", p=128), w=["Kc"])
                        t.dma("pool", Vc[:].rearrange("p (j c) -> p j c", c=128), cv[l, db, :, hc].rearrange("(j p) c -> p j c", p=128), w=["Vc"])
                        for jb in range(4):
                            bank = 6 + (jb % 2)

                            def tr(E, jb=jb, bank=bank):
                                ins = None
                                for jj in range(4):
                                    blk = jb * 4 + jj
                                    ins = E.transpose(out=PS[bank][:, jj * 128:(jj + 1) * 128], in_=Kc[:, blk * 128:(blk + 1) * 128], identity=ident[:, :])
                                return ins
                            t.op("pe", tr, r=["Kc", "ident"], w=[("PS", bank)])
                            if jb % 2 == 0:
                                t.op("act", lambda E, jb=jb, bank=bank: E.activation(out=KcT[:, jb * 512:(jb + 1) * 512], in_=PS[bank][:, :], func=AF.Copy),
                                     r=[("PS", bank)], w=["KcT"] if jb == 0 else [], aw=[] if jb == 0 else ["KcT"])
                            else:
                                t.op("dve", lambda E, jb=jb, bank=bank: E.tensor_copy(out=KcT[:, jb * 512:(jb + 1) * 512], in_=PS[bank][:, :]),
                                     r=[("PS", bank)], aw=["KcT"])
                        qcol = slice(NP + db * 4, NP + db * 4 + 4)

                        def smm(E, db=db, qcol=qcol):
                            ins = None
                            for j in range(16):
                                blk = 15 - j
                                ins = E.matmul(PS[0][:, j * 4:(j + 1) * 4], lhsT=KcT[:, blk * 128:(blk + 1) * 128], rhs=qT[hs][:, qcol], start=True, stop=True)
                            ins = E.matmul(PS[0][0:4, 64:68], lhsT=kT[hs][:, qcol], rhs=qT[hs][:, qcol], start=True, stop=True)
                            return ins
                        t.op("pe", smm, r=["KcT", ("kT", hs), ("qT", hs)], w=[("PS", 0)])
                        t.op("act", lambda E: E.activation(out=Es[:, 0:64], in_=PS[0][:, 0:64], func=AF.Exp, scale=SCALE), r=[("PS", 0)], w=["Es"])
                        t.op("act", lambda E: E.activation(out=Es[0:4, 64:68], in_=PS[0][0:4, 64:68], func=AF.Exp, scale=SCALE), r=[("PS", 0)], pw=["Es"])
                        t.op("dve", lambda E: E.tensor_tensor(out=Ps[:, 0:64].rearrange("p (j c) -> p j c", c=4), in0=Es[:, 0:64].rearrange("p (j c) -> p j c", c=4),
                                                              in1=wbm[hs][:, 512:2560].rearrange("p (j c) -> p j c", c=128)[:, :, 0:4], op=ALU.mult),
                             r=["Es", ("wbm", hs)], w=["Ps"])
                        t.op("dve", lambda E: E.tensor_tensor(out=Ps[0:4, 64:68], in0=Es[0:4, 64:68], in1=wbm[hs][0:4, 384:388], op=ALU.mult),
                             r=["Es", ("wbm", hs)], pw=["Ps"])

                        def spv(E, db=db):
                            ins = None
                            for j in range(16):
                                blk = 15 - j
                                E.matmul(PS[1][:, 0:4], lhsT=Vc[:, blk * 128:(blk + 1) * 128], rhs=Ps[:, j * 4:(j + 1) * 4], start=(j == 0), stop=False)
                            E.matmul(PS[1][:, 0:4], lhsT=vn[hs][db][:, :], rhs=Ps[0:4, 64:68], start=False, stop=True)
                            for j in range(16):
                                E.matmul(PS[1][:, 8:12], lhsT=ones[:, :], rhs=Ps[:, j * 4:(j + 1) * 4], start=(j == 0), stop=False, skip_group_check=True)
                            ins = E.matmul(PS[1][:, 8:12], lhsT=ones[0:4, :], rhs=Ps[0:4, 64:68], start=False, stop=True, skip_group_check=True)
                            return ins
                        t.op("pe", spv, r=["Ps", "Vc", ("vn", hs, db), "ones"], w=[("PS", 1)])
                        t.op("dve", lambda E: E.reciprocal(out=rcs[:, :], in_=PS[1][:, 8:12]), r=[("PS", 1)], w=["rcs"])
                        t.op("dve", lambda E: E.tensor_tensor(out=tms[:, :], in0=PS[1][:, 0:4], in1=rcs[:, :], op=ALU.mult), r=[("PS", 1), "rcs"], w=["tms"])
                        t.op("pool", lambda E, db=db: E.tensor_tensor(out=BIG[:, h * NT + NP + db * 4:h * NT + NP + db * 4 + 4], in0=tms[:, :],
                                                                      in1=sgs[hs][:, db * 4:db * 4 + 4], op=ALU.mult),
                             r=["tms", ("sgs", hs)], pw=["BIG"])
            t.barrier()
            with contextlib.ExitStack() as sc:
                wc = sb(sc, [128, NB * 4], F32)
                prm = sb(sc, [128, 4 * NB], F32)
                c1 = sb(sc, [128, NB], F32)
                c2 = sb(sc, [128, NB], F32)
                etmp = sb(sc, [128, NB], F32)
                wab = sb(sc, [128, NB * 128], BF16)
                wxb = sb(sc, [128, NB * 128], BF16)
                t.dma("sp", wc[:], wconvT[l].rearrange("p n j -> p (n j)"), w=["wc"])
                for i4 in range(4):
                    t.dma("sp", prm[:, i4 * NB:(i4 + 1) * NB], lrup[l, i4], aw=["prm"])
                t.dma("pool", wab[:].rearrange("p (n e) -> p n e", e=128), wa_t[l].rearrange("n d e -> d n e"), w=["wab"])
                t.dma("pool", wxb[:].rearrange("p (n e) -> p n e", e=128), wx_t[l].rearrange("n d e -> d n e"), w=["wxb"])
                t.op("act", lambda E: E.activation(out=etmp[:], in_=prm[:, 3 * NB:4 * NB], func=AF.Exp, scale=-1.0), r=["prm"], w=["etmp"])
                t.op("act", lambda E: E.activation(out=etmp[:], in_=etmp[:], func=AF.Ln, bias=1.0), w=["etmp"])
                t.op("dve", lambda E: E.tensor_scalar(out=c1[:], in0=etmp[:], scalar1=-8.0, scalar2=None, op0=ALU.mult), r=["etmp"], w=["c1"])
                t.op("dve", lambda E: E.tensor_scalar(out=c2[:], in0=etmp[:], scalar1=-16.0, scalar2=None, op0=ALU.mult), r=["etmp"], w=["c2"])
                NBUF = 2
                xe = [sb(sc, [128, 515], F32) for _ in range(NBUF)]
                sgl = [sb(sc, [128, 512], F32) for _ in range(NBUF)]
                xc = [sb(sc, [128, 512], F32) for _ in range(NBUF)]
                xcb = [sb(sc, [128, 512], BF16) for _ in range(NBUF)]
                rr = [sb(sc, [128, 512], F32) for _ in range(NBUF)]
                ig = [sb(sc, [128, 512], F32) for _ in range(NBUF)]
                aa = [sb(sc, [128, 512], F32) for _ in range(NBUF)]
                sq = [sb(sc, [128, 512], F32) for _ in range(NBUF)]
                gx = [sb(sc, [128, 512], F32) for _ in range(NBUF)]
                hh = [sb(sc, [128, 512], F32) for _ in range(NBUF)]
                h0 = sb(sc, [128, 2], F32)
                ci = 0
                for n in range(NB):
                    items = [("p", c, 512) for c in range(4)] + [("s", db, 4) for db in range(2)]
                    prev_h = None
                    for (knd, idx, N) in items:
                        s = ci % NBUF
                        ci += 1
                        if knd == "p":
                            c0 = idx * 512
                            if idx == 0:
                                t.op("dve", lambda E, s=s: E.memset(xe[s][:, 0:3], 0.0), w=[("xe", s)])
                                t.dma("sp", xe[s][:, 3:515], xl_scr[n, :, 0:512], aw=[("xe", s)], key=("xe", s))
                            else:
                                t.dma("sp", xe[s][:, 0:515], xl_scr[n, :, c0 - 3:c0 + 512], w=[("xe", s)])
                            t.dma("sp", sgl[s][:, 0:N], sgl_scr[n, :, c0:c0 + N], w=[("sgl", s)])
                            init = 0.0 if idx == 0 else prev_h
                            ucol = (16 + n) * NT + c0
                        else:
                            db = idx
                            t.dma("sp", xe[s][:, 0:3], sconvT[l, db, n * 128:(n + 1) * 128, :], w=[("xe", s)])
                            t.dma("sp", xe[s][:, 3:7], xl_scr[n, :, NP + db * 4:NP + db * 4 + 4], aw=[("xe", s)], key=("xe", s))
                            t.dma("sp", sgl[s][:, 0:N], sgl_scr[n, :, NP + db * 4:NP + db * 4 + 4], w=[("sgl", s)])
                            t.dma("sp", h0[:, db:db + 1], sh[l, db, n * 128:(n + 1) * 128].rearrange("(p o) -> p o", o=1), w=[("h0", db)])
                            init = h0[:, db:db + 1]
                            ucol = (16 + n) * NT + NP + db * 4
                        w4 = [wc[:, n * 4 + j:n * 4 + j + 1] for j in range(4)]
                        t.op("dve", lambda E, s=s, N=N, w4=w4: E.tensor_scalar(out=xc[s][:, 0:N], in0=xe[s][:, 3:3 + N], scalar1=w4[3], scalar2=prm[:, n:n + 1],
                                                                               op0=ALU.mult, op1=ALU.add), r=[("xe", s), "wc", "prm"], w=[("xc", s)])
                        for j in range(3):
                            t.op("dve", lambda E, s=s, N=N, j=j, w4=w4: E.scalar_tensor_tensor(out=xc[s][:, 0:N], in0=xe[s][:, j:j + N], scalar=w4[j], in1=xc[s][:, 0:N],
                                                                                                op0=ALU.mult, op1=ALU.add), r=[("xe", s)], w=[("xc", s)])
                        t.op("pool", lambda E, s=s, N=N: E.tensor_copy(out=xcb[s][:, 0:N], in_=xc[s][:, 0:N]), r=[("xc", s)], w=[("xcb", s)])
                        pr, pi = (0, 1) if s == 0 else (2, 3)
                        t.op("pe", lambda E, s=s, N=N, pr=pr: E.matmul(PS[pr][:, 0:N], lhsT=wab[:, n * 128:(n + 1) * 128], rhs=xcb[s][:, 0:N], start=True, stop=True),
                             r=["wab", ("xcb", s)], w=[("PS", pr)])
                        t.op("pe", lambda E, s=s, N=N, pi=pi: E.matmul(PS[pi][:, 0:N], lhsT=wxb[:, n * 128:(n + 1) * 128], rhs=xcb[s][:, 0:N], start=True, stop=True),
                             r=["wxb", ("xcb", s)], w=[("PS", pi)])
                        t.op("act", lambda E, s=s, N=N, pr=pr: E.activation(out=rr[s][:, 0:N], in_=PS[pr][:, 0:N], func=AF.Sigmoid, bias=prm[:, NB + n:NB + n + 1]),
                             r=[("PS", pr), "prm"], w=[("rr", s)])
                        t.op("act", lambda E, s=s, N=N, pi=pi: E.activation(out=ig[s][:, 0:N], in_=PS[pi][:, 0:N], func=AF.Sigmoid, bias=prm[:, 2 * NB + n:2 * NB + n + 1]),
                             r=[("PS", pi), "prm"], w=[("ig", s)])
                        t.op("act", lambda E, s=s, N=N: E.activation(out=aa[s][:, 0:N], in_=rr[s][:, 0:N], func=AF.Exp, scale=c1[:, n:n + 1]),
                             r=[("rr", s), "c1"], w=[("aa", s)])
                        t.op("act", lambda E, s=s, N=N: E.activation(out=sq[s][:, 0:N], in_=rr[s][:, 0:N], func=AF.Exp, scale=c2[:, n:n + 1]),
                             r=[("rr", s), "c2"], w=[("sq", s)])
                        t.op("act", lambda E, s=s, N=N: E.activation(out=sq[s][:, 0:N], in_=sq[s][:, 0:N], func=AF.Sqrt, scale=-1.0, bias=1.0),
                             w=[("sq", s)])
                        t.op("pool", lambda E, s=s, N=N: E.tensor_tensor(out=gx[s][:, 0:N], in0=ig[s][:, 0:N], in1=xc[s][:, 0:N], op=ALU.mult),
                             r=[("ig", s), ("xc", s)], w=[("gx", s)])
                        t.op("pool", lambda E, s=s, N=N: E.tensor_tensor(out=gx[s][:, 0:N], in0=gx[s][:, 0:N], in1=sq[s][:, 0:N], op=ALU.mult),
                             r=[("sq", s)], w=[("gx", s)])
                        rds = [("aa", s), ("gx", s)]
                        if knd == "p" and idx > 0:
                            rds.append(("hh", (s - 1) % NBUF))
                        if knd == "s":
                            rds.append(("h0", idx))
                        t.op("dve", lambda E, s=s, N=N, init=init: E.tensor_tensor_scan(out=hh[s][:, 0:N], data0=aa[s][:, 0:N], data1=gx[s][:, 0:N], initial=init,
                                                                                         op0=ALU.mult, op1=ALU.add), r=rds, w=[("hh", s)])
                        prev_h = hh[s][:, N - 1:N]
                        t.op("pool", lambda E, s=s, N=N, ucol=ucol: E.tensor_tensor(out=BIG[:, ucol:ucol + N], in0=hh[s][:, 0:N], in1=sgl[s][:, 0:N], op=ALU.mult),
                             r=[("hh", s), ("sgl", s)], pw=["BIG"])
                        if knd == "p" and idx == 3:
                            t.dma("sp", hp[l, 0, n * 128:(n + 1) * 128].rearrange("(p o) -> p o", o=1), hh[s][:, 511:512], r=[("hh", s)], pw=["hp"], key=("hh", s))
                            t.dma("sp", convT[l, 0, n * 128:(n + 1) * 128, :], xe[s][:, 512:515], r=[("xe", s)], pw=["convT"], key=("xeo", s))
                        if knd == "s":
                            t.dma("sp", hp[l, 1 + idx, n * 128:(n + 1) * 128].rearrange("(p o) -> p o", o=1), hh[s][:, 3:4], r=[("hh", s)], pw=["hp"], key=("hh", s))
                            t.dma("sp", convT[l, 1 + idx, n * 128:(n + 1) * 128, :], xe[s][:, 4:7], r=[("xe", s)], pw=["convT"], key=("xeo", s))

        def phase_e(l, xsrc, xdst):
            with contextlib.ExitStack() as sc:
                GGp = sb(sc, [128, D], F32)
                GGs = sb(sc, [NS, D], F32)
                gpb = sb(sc, [128, D], F32)
                yt = [sb(sc, [128, D], F32) for _ in range(2)]
                xt = [sb(sc, [128, D], F32) for _ in range(2)]
                junk = sb(sc, [128, D], BF16)
                stat = sb(sc, [128, 17 * 4], F32)
                t.dma("sp", GGp[:], bass.AP(mod_h, (l * 3 + 0) * D, [[0, 128], [1, D]]), w=["GGp"])
                t.dma("sp", GGs[0:4, :], bass.AP(mod_h, (l * 3 + 1) * D, [[0, 4], [1, D]]), w=["GGs"])
                t.dma("sp", GGs[4:8, :], bass.AP(mod_h, (l * 3 + 2) * D, [[0, 4], [1, D]]), aw=["GGs"], key="GGs")
                t.dma("sp", gpb[:], bass.AP(gpost_h, l * D, [[0, 128], [1, D]]), w=["gpb"])
                t.op("dve", lambda E: E.tensor_tensor(out=GGp[:], in0=GGp[:], in1=gpb[:], op=ALU.mult), r=["gpb"], w=["GGp"])
                t.op("dve", lambda E: E.tensor_tensor(out=GGs[:], in0=GGs[:], in1=gpb[0:NS, :], op=ALU.mult), r=["gpb"], w=["GGs"])
                for tt in range(17):
                    M = 128 if tt < 16 else NS
                    r0 = tt * 128
                    s = tt % 2
                    so = tt * 4
                    GG = GGp if tt < 16 else GGs
                    gk = "GGp" if tt < 16 else "GGs"
                    t.dma("sp", yt[s][0:M, :], y_scr[r0:r0 + M, :], w=[("yt", s)])
                    t.dma("sp", xt[s][0:M, :], xsrc[r0:r0 + M, :], w=[("xt", s)])
                    t.op("act", lambda E, s=s, M=M, so=so: E.activation(out=junk[0:M, :], in_=yt[s][0:M, :], func=AF.Square, accum_out=stat[0:M, so:so + 1]),
                         r=[("yt", s)], w=["junk", ("st", 0)])
                    t.op("dve", lambda E, M=M, so=so: E.tensor_scalar(out=stat[0:M, so + 1:so + 2], in0=stat[0:M, so:so + 1], scalar1=1.0 / D, scalar2=EPS,
                                                                       op0=ALU.mult, op1=ALU.add), r=[("st", 0)], w=[("st", 1)])
                    t.op("act", lambda E, M=M, so=so: E.activation(out=stat[0:M, so + 2:so + 3], in_=stat[0:M, so + 1:so + 2], func=AF.Sqrt),
                         r=[("st", 1)], w=[("st", 2)])
                    t.op("dve", lambda E, M=M, so=so: E.reciprocal(out=stat[0:M, so + 3:so + 4], in_=stat[0:M, so + 2:so + 3]), r=[("st", 2)], w=[("st", 3)])
                    t.op("dve", lambda E, s=s, M=M, so=so, GG=GG: E.scalar_tensor_tensor(out=yt[s][0:M, :], in0=yt[s][0:M, :], scalar=stat[0:M, so + 3:so + 4],
                                                                                         in1=GG[0:M, :], op0=ALU.mult, op1=ALU.mult),
                         r=[("st", 3), gk], w=[("yt", s)])
                    t.op("pool", lambda E, s=s, M=M: E.tensor_tensor(out=xt[s][0:M, :], in0=xt[s][0:M, :], in1=yt[s][0:M, :], op=ALU.add),
                         r=[("yt", s)], w=[("xt", s)])
                    t.dma("sp", xdst[r0:r0 + M, :], xt[s][0:M, :], r=[("xt", s)], pw=["xdst"], key=("xto", s))

        if stop <= 1:
            return nc
        for l in range(DEPTH):
            xsrc = x_in if l == 0 else x1_scr
            xdst = x1_scr if l == 0 else yp
            with contextlib.ExitStack() as sc:
                BIG = sb(sc, [128, 32 * NT], BF16, "BIGh")
                phase_a(l, xsrc, BIG)
                t.barrier()
                if stop <= 2:
                    return nc
                proj(l, win[l], NGI, kind_in, BIG)
                t.barrier()
            if stop <= 3:
                return nc
            with contextlib.ExitStack() as sc:
                BIG = sb(sc, [128, 32 * NT], BF16, "BIGu")
                phase_c(l, BIG)
                t.barrier()
                if stop <= 4:
                    return nc
                proj(l, wout[l], NGO, lambda cbi: "y", BIG)
                t.barrier()
            if stop <= 5:
                return nc
            phase_e(l, xsrc, xdst)
            t.barrier()
            if stop <= 6:
                return nc
    return nc


def _bucket(i):
    if i < 16:
        return i
    d = np.float32(i)
    v = np.float32(16) + np.log(np.maximum(d, np.float32(1.0)) / np.float32(16)) / np.float32(math.log(2048 / 16)) * np.float32(16)
    return int(min(int(np.float32(v)), 31))


def _consts():
    cp = np.zeros((32, TVL), np.float32)
    for i in range(0, 2049):
        b = _bucket(i)
        cnt = 0
        for (wdw, dil) in ((128, 1), (512, 4), (2048, 16)):
            if i % dil == 0 and i // dil <= wdw // dil:
                cnt += 1
        if cnt:
            cp[b, i + TOFF] += cnt
    jm = np.eye(128, dtype=np.float32)[::-1].copy()
    return cp, jm, np.eye(128, dtype=np.float32)


_NC_CACHE = {}
_NCORES = [8]
_STOP = [99]
_DBG = {'ntt': 17, 'evac': True}


def kernel(x_prompt, x_sample, cache_k, cache_v, state_h, state_conv, c_prompt, c_sample, rel_table, w_ada, b_ada,
           g_pre, w_in, w_conv, b_conv, w_a, b_a, w_x, b_x, lam, w_out, g_post):
    f = lambda a: np.ascontiguousarray(np.asarray(a, dtype=np.float32))
    x_prompt, x_sample, cache_k, cache_v = f(x_prompt), f(x_sample), f(cache_k), f(cache_v)
    state_h, state_conv, c_prompt, c_sample = f(state_h), f(state_conv), f(c_prompt), f(c_sample)
    cp, jm, idn = _consts()

    def tile_w(w, ng):
        L = w.shape[0]
        return np.ascontiguousarray(f(w).reshape(L, 32, 128, ng, 256).transpose(0, 3, 2, 1, 4)).reshape(L, ng, 128, 8192)
    shared = {
        "rel": f(rel_table), "cpad": cp, "jmat": jm, "ident": idn,
        "wada": tile_w(w_ada, NGI), "bada": f(b_ada),
        "gpreT": np.ascontiguousarray(f(g_pre).reshape(DEPTH, 32, 128).transpose(0, 2, 1)),
        "win": tile_w(w_in, NGI), "wout": tile_w(w_out, NGO),
        "wconvT": np.ascontiguousarray(f(w_conv).reshape(DEPTH, 4, NB, 128).transpose(0, 3, 2, 1)),
        "lrup": np.ascontiguousarray(np.stack([f(b_conv), f(b_a), f(b_x), f(lam)], axis=1).reshape(DEPTH, 4, NB, 128).transpose(0, 1, 3, 2)),
        "wa_t": f(w_a), "wx_t": f(w_x), "gpost": f(g_post),
    }
    in_maps = []
    for c in range(_NCORES[0]):
        b = c % 4
        m = dict(shared)
        m["x_in"] = np.ascontiguousarray(np.concatenate([x_prompt[b], x_sample[2 * b], x_sample[2 * b + 1]], axis=0))
        m["c3"] = np.ascontiguousarray(np.stack([c_prompt[b], c_sample[2 * b], c_sample[2 * b + 1]], axis=0))
        m["ck"] = np.ascontiguousarray(cache_k[:, 2 * b:2 * b + 2].reshape(DEPTH, 2, NP, H * 128))
        m["cv"] = np.ascontiguousarray(cache_v[:, 2 * b:2 * b + 2].reshape(DEPTH, 2, NP, H * 128))
        m["sh"] = np.ascontiguousarray(state_h[:, 2 * b:2 * b + 2])
        m["sconvT"] = np.ascontiguousarray(state_conv[:, 2 * b:2 * b + 2].transpose(0, 1, 3, 2))
        in_maps.append(m)
    if "nc" not in _NC_CACHE:
        _NC_CACHE["nc"] = build_nc(_STOP[0])
    res = run_bass_kernel_spmd(_NC_CACHE["nc"], in_maps, core_ids=list(range(_NCORES[0])))
    R = res.results
    if _NCORES[0] < 8:
        R = [R[c % _NCORES[0]] for c in range(8)]
    B, DB = 4, 8
    y_p = np.zeros((B, NP, D), np.float32)
    y_s = np.zeros((DB, 4, D), np.float32)
    k_p = np.zeros((DEPTH, B, NP, H, 128), np.float32)
    v_p = np.zeros_like(k_p)
    h_p = np.zeros((DEPTH, B, 2048), np.float32)
    c_p = np.zeros((DEPTH, B, 3, 2048), np.float32)
    k_s = np.zeros((DEPTH, DB, 4, H, 128), np.float32)
    v_s = np.zeros_like(k_s)
    h_s = np.zeros((DEPTH, DB, 2048), np.float32)
    c_s = np.zeros((DEPTH, DB, 3, 2048), np.float32)
    for b in range(B):
        r = R[b]
        y_p[b] = r["yp"][:NP]
        kpr, vpr, hpr, cvr = r["kp"], r["vp"], r["hp"], r["convT"]
        k_p[:, b] = kpr[:, :NP].reshape(DEPTH, NP, H, 128)
        v_p[:, b] = vpr[:, :NP].reshape(DEPTH, NP, H, 128)
        h_p[:, b] = hpr[:, 0]
        c_p[:, b] = cvr[:, 0].transpose(0, 2, 1)
        for db in range(2):
            y_s[2 * b + db] = r["yp"][NP + 4 * db:NP + 4 * db + 4]
            k_s[:, 2 * b + db] = kpr[:, NP + 4 * db:NP + 4 * db + 4].reshape(DEPTH, 4, H, 128)
            v_s[:, 2 * b + db] = vpr[:, NP + 4 * db:NP + 4 * db + 4].reshape(DEPTH, 4, H, 128)
            h_s[:, 2 * b + db] = hpr[:, 1 + db]
            c_s[:, 2 * b + db] = cvr[:, 1 + db].transpose(0, 2, 1)
    return (y_p, y_s, k_p, v_p, h_p, c_p, k_s, v_s, h_s, c_s)
```

```python
import contextlib
import math
import numpy as np
import concourse.bass as bass
import concourse.mybir as mybir
from concourse.bass_utils import run_bass_kernel_spmd

F32, BF16 = mybir.dt.float32, mybir.dt.bfloat16
AF = mybir.ActivationFunctionType
ALU = mybir.AluOpType

D = 4096
NP = 2048
NS = 8
NT = NP + NS
NTP = 2080
H = 16
NB = 16
EPS = 1e-6
SCALE = 128 ** -0.5
TW = 2560
TVL = 2688
TOFF = 511
DEPTH = 2
NGI = 48
NGO = 16


class Sem:
    pass


class Trk:
    def __init__(self, nc, es):
        self.nc, self.es = nc, es
        self.eng = {"pe": nc.tensor, "act": nc.scalar, "dve": nc.vector, "pool": nc.gpsimd, "sp": nc.sync}
        self.selfsem = {}
        self.allsems = []
        self.W, self.R = {}, {}
        self.seen = {k: {} for k in self.eng}
        self.dmasem = {}
        self.cnt = 0
        self.epoch()

    def newsem(self, name):
        s = Sem()
        self.cnt += 1
        s.h = self.es.enter_context(self.nc.semaphore(f"{name}{self.cnt}"))
        s.n = 0
        self.allsems.append(s)
        return s

    def epoch(self):
        for k in self.eng:
            self.selfsem[k] = self.newsem("e" + k)

    def _wait(self, e, sem, val):
        if val <= 0 or self.seen[e].get(sem, 0) >= val:
            return
        self.eng[e].wait_ge(sem.h, val)
        self.seen[e][sem] = val

    def op(self, e, fn, r=(), w=(), pw=(), aw=(), sem=None, k=1):
        waits = {}

        def add(d):
            for s, v in d.items():
                if waits.get(s, 0) < v:
                    waits[s] = v
        for b in r:
            add(self.W.get(b, {}))
            if isinstance(b, tuple) and b[0] == "PS":
                for s_, v_ in self.R.get(b, {}).items():
                    if s_ is not self.selfsem[e] and waits.get(s_, 0) < v_:
                        waits[s_] = v_
        for b in list(w) + list(aw):
            add(self.W.get(b, {}))
            add(self.R.get(b, {}))
        for s, v in waits.items():
            self._wait(e, s, v)
        ins = fn(self.eng[e])
        if sem is None:
            sem = self.selfsem[e]
        ins.then_inc(sem.h, k)
        sem.n += k
        for b in w:
            self.W[b] = {sem: sem.n}
            self.R[b] = {}
        for b in list(pw) + list(aw):
            self.W.setdefault(b, {})[sem] = sem.n
        for b in r:
            self.R.setdefault(b, {})[sem] = sem.n
        return (sem, sem.n)

    def dma(self, q, out, in_, r=(), w=(), pw=(), aw=(), key=None, **kw):
        if key is None:
            key = w[0] if w else (aw[0] if aw else (pw[0] if pw else r[0]))
        ds = self.dmasem.get(key)
        if ds is None:
            ds = self.dmasem[key] = self.newsem("d")
        return self.op(q, lambda E: E.dma_start(out=out, in_=in_, **kw), r=r, w=w, pw=pw, aw=aw, sem=ds, k=16)

    def barrier(self):
        for e in self.eng:
            for s in self.allsems:
                self._wait(e, s, s.n)
        self.W.clear()
        self.R.clear()


def build_nc(stop=99):
    nc = bass.Bass("TRN2", target_bir_lowering=False)

    def din(name, shape, dt=F32):
        return nc.dram_tensor(name, shape, dt, kind="ExternalInput")

    def dout(name, shape, dt=F32):
        return nc.dram_tensor(name, shape, dt, kind="ExternalOutput")

    def dscr(name, shape, dt=F32):
        return nc.dram_tensor(name, shape, dt, kind="Internal")

    x_in = din("x_in", [NT, D]).ap()
    c3 = din("c3", [3, D]).ap()
    ck = din("ck", [DEPTH, 2, NP, H * 128]).ap()
    cv = din("cv", [DEPTH, 2, NP, H * 128]).ap()
    sh = din("sh", [DEPTH, 2, 2048]).ap()
    sconvT = din("sconvT", [DEPTH, 2, 2048, 3]).ap()
    rel = din("rel", [32, 16]).ap()
    cpad = din("cpad", [32, TVL]).ap()
    jmat_d = din("jmat", [128, 128]).ap()
    ident_d = din("ident", [128, 128]).ap()
    wada = din("wada", [DEPTH, NGI, 128, 8192]).ap()
    bada_h = din("bada", [DEPTH, 12288])
    gpreT = din("gpreT", [DEPTH, 128, 32]).ap()
    win = din("win", [DEPTH, NGI, 128, 8192]).ap()
    wout = din("wout", [DEPTH, NGO, 128, 8192]).ap()
    wconvT = din("wconvT", [DEPTH, 128, NB, 4]).ap()
    lrup = din("lrup", [DEPTH, 4, 128, NB]).ap()
    wa_t = din("wa_t", [DEPTH, NB, 128, 128]).ap()
    wx_t = din("wx_t", [DEPTH, NB, 128, 128]).ap()
    gpost_h = din("gpost", [DEPTH, D])

    yp = dout("yp", [NT, D]).ap()
    kp = dout("kp", [DEPTH, NT, H * 128]).ap()
    vp = dout("vp", [DEPTH, NT, H * 128]).ap()
    hp = dout("hp", [DEPTH, 3, 2048]).ap()
    convT = dout("convT", [DEPTH, 3, 2048, 3]).ap()

    qT_scr = dscr("qT_scr", [H, 128, NTP], BF16).ap()
    kT_scr = dscr("kT_scr", [H, 128, NTP], BF16).ap()
    vb_scr = dscr("vb_scr", [NT, H * 128], BF16).ap()
    sg_scr = dscr("sg_scr", [H, 128, NTP]).ap()
    xl_scr = dscr("xl_scr", [NB, 128, NTP]).ap()
    sgl_scr = dscr("sgl_scr", [NB, 128, NTP]).ap()
    y_scr = dscr("y_scr", [NT, D]).ap()
    x1_scr = dscr("x1_scr", [NT, D]).ap()
    mod_h = dscr("mod_scr", [DEPTH, 3, D])
    mod_scr = mod_h.ap()
    tv_h = dscr("tv_scr", [H, TVL])
    tv_scr = tv_h.ap()
    wb_scr = dscr("wb_scr", [H, 128, TW], BF16).ap()
    u_scr = dscr("u_scr", [32, 128, NTP], BF16).ap()

    uid = [0]

    def sb(scope, shape, dt, name="t"):
        uid[0] += 1
        return scope.enter_context(nc.sbuf_tensor(f"{name}{uid[0]}", shape, dt))

    with contextlib.ExitStack() as es:
        t = Trk(nc, es)
        PS = [es.enter_context(nc.psum_tensor(f"ps{i}", [128, 512], F32)) for i in range(8)]
        ident = sb(es, [128, 128], F32, "ident")
        jmat = sb(es, [128, 128], F32, "jmat")
        ones = sb(es, [128, 128], BF16, "ones")
        AB = sb(es, [128, DEPTH * 2 * 32 * 3], F32, "AB")
        scT = sb(es, [128, 96], BF16, "scT")

        def ABap(l, ab, kt, g):
            o = ((l * 2 + ab) * 32 + kt) * 3 + g
            return AB[:, o:o + 1]

        t.dma("sp", ident[:], ident_d, w=["ident"])
        t.dma("sp", jmat[:], jmat_d, w=["jmat"])
        t.op("dve", lambda E: E.memset(ones[:], 1.0), w=["ones"])

        with contextlib.ExitStack() as sc:
            rel_sb = sb(sc, [32, 16], F32)
            cp_sb = sb(sc, [32, TVL], F32)
            et = sb(sc, [32, 16], F32)
            tv_sb = sb(sc, [16, TVL], F32)
            hk = [sb(sc, [128, TW], F32) for _ in range(2)]
            wbm = [sb(sc, [128, TW], BF16) for _ in range(2)]
            t.dma("sp", rel_sb[:], rel, w=["rel"])
            t.dma("sp", cp_sb[:], cpad, w=["cp"])
            t.op("act", lambda E: E.activation(out=et[:], in_=rel_sb[:], func=AF.Exp), r=["rel"], w=["et"])
            for ch in range(6):
                n = min(512, TVL - ch * 512)
                t.op("pe", lambda E, ch=ch, n=n: E.matmul(PS[ch][0:16, 0:n], lhsT=et[:], rhs=cp_sb[:, ch * 512:ch * 512 + n],
                                                           start=True, stop=True), r=["et", "cp"], w=[("PS", ch)])
                t.op("dve", lambda E, ch=ch, n=n: E.tensor_copy(out=tv_sb[:, ch * 512:ch * 512 + n], in_=PS[ch][0:16, 0:n]),
                     r=[("PS", ch)], pw=["tv"])
            t.dma("sp", tv_scr, tv_sb[:], r=["tv"], w=["tv_scr"])
            bk = 0
            for h in range(H):
                s = h % 2
                src = bass.AP(tv_h, h * TVL, [[1, 128], [1, TW]])
                t.dma("sp", hk[s][:], src, r=["tv_scr"], w=[("hk", s)])
                for ch in range(5):
                    b = bk % 8
                    bk += 1
                    t.op("pe", lambda E, s=s, ch=ch, b=b: E.matmul(PS[b][:, :], lhsT=jmat[:], rhs=hk[s][:, ch * 512:(ch + 1) * 512],
                                                                    start=True, stop=True), r=["jmat", ("hk", s)], w=[("PS", b)])
                    first = (ch == 0)
                    if ch % 2 == 0:
                        t.op("act", lambda E, s=s, ch=ch, b=b: E.activation(out=wbm[s][:, ch * 512:(ch + 1) * 512], in_=PS[b][:, :], func=AF.Copy),
                             r=[("PS", b)], w=[("wbm", s)] if first else [], aw=[] if first else [("wbm", s)])
                    else:
                        t.op("dve", lambda E, s=s, ch=ch, b=b: E.tensor_copy(out=wbm[s][:, ch * 512:(ch + 1) * 512], in_=PS[b][:, :]),
                             r=[("PS", b)], aw=[("wbm", s)])
                t.dma("sp", wb_scr[h], wbm[s][:], r=[("wbm", s)], pw=["wb_scr"])
        t.barrier()
        if stop <= 0:
            return nc

        def ada_prep():
            with contextlib.ExitStack() as sc:
                c3s = sb(sc, [3, D], F32)
                t.dma("sp", c3s[:], c3, w=["c3s"])
                t.op("act", lambda E: E.activation(out=c3s[:], in_=c3s[:], func=AF.Silu), w=["c3s"])
                for kt in range(32):
                    t.op("pe", lambda E, kt=kt: E.transpose(out=PS[0][:, kt * 3:(kt + 1) * 3], in_=c3s[:, kt * 128:(kt + 1) * 128],
                                                            identity=ident[0:3, 0:3]), r=["c3s", "ident"],
                         w=[("PS", 0)] if kt == 0 else [], pw=[] if kt == 0 else [("PS", 0)])
                t.op("dve", lambda E: E.tensor_copy(out=scT[:], in_=PS[0][:, 0:96]), r=[("PS", 0)], w=["scT"])
                t.barrier()

        def ada_gen(l, sc, pbank):
            Wb = [sb(sc, [128, 8192], BF16) for _ in range(2)]
            bb = [sb(sc, [3, 256], F32) for _ in range(2)]
            modg = [sb(sc, [3, 256], F32) for _ in range(2)]
            modT = sb(sc, [128, 192], F32)
            gpr = sb(sc, [128, 32], F32)
            tmp1 = sb(sc, [128, 96], F32)
            for g in range(NGI):
                s = g % 2
                pb = pbank
                t.dma("pool", Wb[s][:], wada[l, g], w=[("adaW", s)], max_dma_last_dim=8192)
                t.dma("sp", bb[s][:], bass.AP(bada_h, l * 12288 + g * 256, [[0, 3], [1, 256]]), w=[("bb", s)])

                def mm(E, s=s, pb=pb):
                    ins = None
                    for kt in range(32):
                        ins = E.matmul(PS[pb][0:3, 0:256], lhsT=scT[:, kt * 3:(kt + 1) * 3], rhs=Wb[s][:, kt * 256:(kt + 1) * 256],
                                       start=(kt == 0), stop=(kt == 31))
                    return ins
                t.op("pe", mm, r=[("adaW", s), "scT"], w=[("PS", pb)])
                t.op("dve", lambda E, s=s, pb=pb: E.tensor_tensor(out=modg[s][:], in0=PS[pb][0:3, 0:256], in1=bb[s][:], op=ALU.add),
                     r=[("PS", pb), ("bb", s)], w=[("modg", s)])
                if g < 32:
                    def tr(E, s=s, pb=pb):
                        ins = None
                        for cb in range(2):
                            ins = E.transpose(out=PS[pb][:, 256 + cb * 3:259 + cb * 3], in_=modg[s][:, cb * 128:(cb + 1) * 128], identity=ident[0:3, 0:3])
                        return ins
                    t.op("pe", tr, r=[("modg", s), "ident"], w=[("PS", pb)])
                    t.op("dve", lambda E, g=g, pb=pb: E.tensor_copy(out=modT[:, g * 6:g * 6 + 6], in_=PS[pb][:, 256:262]), r=[("PS", pb)], pw=["modT"])
                else:
                    t.dma("sp", mod_scr[l, :, (g - 32) * 256:(g - 31) * 256], modg[s][:], r=[("modg", s)], pw=["mod_scr"])
                yield
            t.dma("sp", gpr[:], gpreT[l], w=["gpr"])
            t.op("dve", lambda E: E.tensor_scalar(out=tmp1[:], in0=modT[:, 96:192], scalar1=1.0, scalar2=None, op0=ALU.add),
                 r=["modT"], w=["tmp1"])
            oa = (l * 2 + 0) * 96
            ob = (l * 2 + 1) * 96
            for g3 in range(3):
                t.op("dve", lambda E, g3=g3: E.tensor_tensor(out=AB[:, oa:oa + 96].rearrange("p (k g) -> p k g", g=3)[:, :, g3],
                                                              in0=tmp1[:].rearrange("p (k g) -> p k g", g=3)[:, :, g3], in1=gpr[:], op=ALU.mult),
                     r=["tmp1", "gpr"], pw=["AB"])
            t.op("dve", lambda E: E.tensor_copy(out=AB[:, ob:ob + 96], in_=modT[:, 0:96]), r=["modT"], pw=["AB"])
            yield

        def run_gens(gens):
            alive = list(gens)
            while alive:
                for g in list(alive):
                    try:
                        next(g)
                    except StopIteration:
                        alive.remove(g)

        ada_prep()
        with contextlib.ExitStack() as sc:
            run_gens([ada_gen(0, sc, 1)])
        t.barrier()
        if stop <= 1:
            return nc

        def phase_a(l, xsrc, BIG):
            with contextlib.ExitStack() as sc:
                xa = [sb(sc, [128, D], F32) for _ in range(2)]
                junk = sb(sc, [128, D], BF16)
                stat = sb(sc, [128, 17 * 4], F32)
                diag = [sb(sc, [128, 128], F32) for _ in range(2)]
                ei = 0
                bi = 0
                for tt in range(_DBG['ntt']):
                    M = 128 if tt < 16 else NS
                    r0 = tt * 128
                    s = tt % 2
                    so = tt * 4
                    t.dma("sp", xa[s][0:M, :], xsrc[r0:r0 + M, :], w=[("xa", s)])
                    t.op("act", lambda E, s=s, M=M, so=so: E.activation(out=junk[0:M, :], in_=xa[s][0:M, :], func=AF.Square,
                                                                         accum_out=stat[0:M, so:so + 1]),
                         r=[("xa", s)], w=["junk", ("st", 0)])
                    t.op("dve", lambda E, M=M, so=so: E.tensor_scalar(out=stat[0:M, so + 1:so + 2], in0=stat[0:M, so:so + 1], scalar1=1.0 / D,
                                                                       scalar2=EPS, op0=ALU.mult, op1=ALU.add), r=[("st", 0)], w=[("st", 1)])
                    t.op("act", lambda E, M=M, so=so: E.activation(out=stat[0:M, so + 2:so + 3], in_=stat[0:M, so + 1:so + 2], func=AF.Sqrt),
                         r=[("st", 1)], w=[("st", 2)])
                    t.op("dve", lambda E, M=M, so=so: E.reciprocal(out=stat[0:M, so + 3:so + 4], in_=stat[0:M, so + 2:so + 3]),
                         r=[("st", 2)], w=[("st", 3)])
                    t.op("dve", lambda E, s=s, M=M, so=so: E.tensor_scalar(out=diag[s][0:M, 0:M], in0=ident[0:M, 0:M], scalar1=stat[0:M, so + 3:so + 4],
                                                                            scalar2=None, op0=ALU.mult), r=[("st", 3), "ident"], w=[("dg", s)])
                    for q4 in range(8):
                        b = bi % 6
                        bi += 1

                        def mm(E, s=s, M=M, q4=q4, b=b):
                            ins = None
                            for j in range(4):
                                kt = q4 * 4 + j
                                ins = E.matmul(PS[b][:, j * 128:j * 128 + M], lhsT=xa[s][0:M, kt * 128:(kt + 1) * 128], rhs=diag[s][0:M, 0:M],
                                               start=True, stop=True)
                            return ins
                        t.op("pe", mm, r=[("xa", s), ("dg", s)], w=[("PS", b)])
                        for j in range(4):
                            kt = q4 * 4 + j
                            grps = [(0, M, 0)] if tt < 16 else [(0, 4, 1), (4, 8, 2)]
                            for (c0, c1, g3) in grps:
                                ei += 1
                                dst = BIG[:, kt * NT + r0 + c0:kt * NT + r0 + c1]
                                srcp = PS[b][:, j * 128 + c0:j * 128 + c1]
                                if b % 2 == 0:
                                    t.op("act", lambda E, dst=dst, srcp=srcp, kt=kt, g3=g3: E.activation(
                                        out=dst, in_=srcp, func=AF.Identity, scale=ABap(l, 0, kt, g3), bias=ABap(l, 1, kt, g3)),
                                        r=[("PS", b), "AB"], pw=["BIG"])
                                else:
                                    t.op("dve", lambda E, dst=dst, srcp=srcp, kt=kt, g3=g3: E.tensor_scalar(
                                        out=dst, in0=srcp, scalar1=ABap(l, 0, kt, g3), scalar2=ABap(l, 1, kt, g3), op0=ALU.mult, op1=ALU.add),
                                        r=[("PS", b), "AB"], pw=["BIG"])

        def proj(l, wsrc, ng, kind_of, BIG):
            with contextlib.ExitStack() as sc:
                Wb = [sb(sc, [128, 8192], BF16) for _ in range(2)]
                stF = [sb(sc, [128, 1032], F32) for _ in range(2)]
                stH = [sb(sc, [128, 1032], BF16) for _ in range(2)]
                sT = [sb(sc, [128, 512], F32) for _ in range(2)]
                sTb = [sb(sc, [128, 512], BF16) for _ in range(2)]
                setbanks = [(0, 1, 2), (3, 4, 5)]
                st = {"si": 0, "tb": 0}

                def epi_pe(info):
                    kind, cbi, h2, s = info
                    if kind not in ("k", "v", "y"):
                        return
                    tok0 = h2 * 1024
                    tiles = [(j, 128) for j in range(8)] + ([(8, NS)] if h2 == 1 else [])
                    for jb in range(0, len(tiles), 4):
                        batch = tiles[jb:jb + 4]
                        tbk = st["tb"] % 2
                        st["tb"] += 1
                        bank = 6 + tbk

                        def tr(E, batch=batch, s=s, bank=bank):
                            ins = None
                            for jj, (j, M) in enumerate(batch):
                                ins = E.transpose(out=PS[bank][0:M, jj * 128:(jj + 1) * 128], in_=stF[s][:, j * 128:j * 128 + M], identity=ident[:, :])
                            return ins
                        t.op("pe", tr, r=[("stF", s), "ident"], w=[("PS", bank)])
                        M = batch[0][1]
                        nj = len(batch)
                        t.op("dve", lambda E, tbk=tbk, bank=bank, M=M, nj=nj: E.tensor_copy(out=sT[tbk][0:M, 0:nj * 128], in_=PS[bank][0:M, 0:nj * 128]),
                             r=[("PS", bank)], w=[("sT", tbk)])
                        r0 = tok0 + batch[0][0] * 128
                        nrow = (nj - 1) * 128 + M
                        if kind == "y":
                            dst = y_scr[r0:r0 + nrow, cbi * 128:(cbi + 1) * 128]
                        elif kind == "k":
                            dst = kp[l, r0:r0 + nrow, (cbi - 16) * 128:(cbi - 15) * 128]
                        else:
                            dst = vp[l, r0:r0 + nrow, (cbi - 32) * 128:(cbi - 31) * 128]
                        if M == 128:
                            dst = dst.rearrange("(j p) c -> p j c", p=128)
                            srcv = sT[tbk][:, 0:nj * 128].rearrange("p (j c) -> p j c", c=128)
                        else:
                            srcv = sT[tbk][0:M, 0:128]
                        t.dma("sp", dst, srcv, r=[("sT", tbk)], pw=["outkvy"], key=("sT", tbk))
                        if kind == "v":
                            t.op("pool", lambda E, tbk=tbk, M=M, nj=nj: E.tensor_copy(out=sTb[tbk][0:M, 0:nj * 128], in_=sT[tbk][0:M, 0:nj * 128]),
                                 r=[("sT", tbk)], w=[("sTb", tbk)])
                            dstb = vb_scr[r0:r0 + nrow, (cbi - 32) * 128:(cbi - 31) * 128]
                            if M == 128:
                                dstb = dstb.rearrange("(j p) c -> p j c", p=128)
                                srcb = sTb[tbk][:, 0:nj * 128].rearrange("p (j c) -> p j c", c=128)
                            else:
                                srcb = sTb[tbk][0:M, 0:128]
                            t.dma("sp", dstb, srcb, r=[("sTb", tbk)], pw=["vb_scr"], key=("sTb", tbk))

                prev = None
                for g in range(ng):
                    ws = g % 2
                    t.dma("pool", Wb[ws][:], wsrc[g], w=[("W", ws)], max_dma_last_dim=8192)
                    for cb in range(2):
                        cbi = g * 2 + cb
                        kind = kind_of(cbi)
                        for h2 in range(2):
                            si = st["si"]
                            st["si"] += 1
                            s = si % 2
                            bs = setbanks[s]
                            tok0 = h2 * 1024
                            chunks = [(0, 512), (512, 512)] + ([(1024, NS)] if h2 == 1 else [])

                            def mm(E, ws=ws, cb=cb, tok0=tok0, chunks=chunks, bs=bs):
                                ins = None
                                for kt in range(32):
                                    for j, (c0, n) in enumerate(chunks):
                                        ins = E.matmul(PS[bs[j]][:, 0:n], lhsT=Wb[ws][:, kt * 256 + cb * 128:kt * 256 + (cb + 1) * 128],
                                                       rhs=BIG[:, kt * NT + tok0 + c0:kt * NT + tok0 + c0 + n], start=(kt == 0), stop=(kt == 31))
                                return ins
                            t.op("pe", mm, r=[("W", ws), "BIG"], w=[("PS", bs[j]) for j in range(len(chunks))])
                            tgt, tkey = (stH[s], ("stH", s)) if kind == "q" else (stF[s], ("stF", s))
                            fn = AF.Silu if kind in ("ga", "gl") else AF.Copy
                            for j, (c0, n) in enumerate(chunks):
                                t.op("act", lambda E, tgt=tgt, c0=c0, n=n, bj=bs[j], fn=fn: E.activation(out=tgt[:, c0:c0 + n], in_=PS[bj][:, 0:n], func=fn),
                                     r=[("PS", bs[j])], w=[tkey] if j == 0 else [], pw=[] if j == 0 else [tkey])
                            ntok = chunks[-1][0] + chunks[-1][1]
                            hh = cbi % 16
                            if kind == "q":
                                t.dma("sp", qT_scr[hh, :, tok0:tok0 + ntok], stH[s][:, 0:ntok], r=[tkey], pw=["qT_scr"], key=tkey)
                            elif kind == "k":
                                t.op("dve", lambda E, s=s, ntok=ntok: E.tensor_copy(out=stH[s][:, 0:ntok], in_=stF[s][:, 0:ntok]),
                                     r=[("stF", s)], w=[("stH", s)])
                                t.dma("sp", kT_scr[hh, :, tok0:tok0 + ntok], stH[s][:, 0:ntok], r=[("stH", s)], pw=["kT_scr"], key=("stH", s))
                            elif kind in ("ga", "xl", "gl"):
                                dsc = {"ga": sg_scr, "xl": xl_scr, "gl": sgl_scr}[kind]
                                t.dma("sp", dsc[hh, :, tok0:tok0 + ntok], stF[s][:, 0:ntok], r=[tkey], pw=["scr_" + kind], key=tkey)
                            if prev is not None:
                                epi_pe(prev)
                            prev = (kind, cbi, h2, s)
                epi_pe(prev)

        def kind_in(cbi):
            return ["q", "k", "v", "ga", "xl", "gl"][cbi // 16]

        def phase_c(l, do_ada_next):
            with contextlib.ExitStack() as sc:
                def g_attn():
                    qT = [sb(sc, [128, NP], BF16) for _ in range(2)]
                    kT = [sb(sc, [128, NP], BF16) for _ in range(2)]
                    V = [sb(sc, [128, 2048], BF16) for _ in range(2)]
                    wbm = [sb(sc, [128, TW], BF16) for _ in range(2)]
                    sgc = [sb(sc, [128, 512], F32) for _ in range(2)]
                    Eb = [sb(sc, [128, 512], BF16) for _ in range(2)]
                    Pb = [sb(sc, [128, 512], BF16) for _ in range(2)]
                    lnd = [sb(sc, [128, 512], F32) for _ in range(2)]
                    tmpo = [sb(sc, [128, 512], F32) for _ in range(2)]
                    uo = [sb(sc, [128, 512], BF16) for _ in range(2)]
                    qi = 0
                    ii = 0
                    for h in range(H):
                        hs = h % 2
                        hc = slice(h * 128, (h + 1) * 128)
                        t.dma("sp", qT[hs][:], qT_scr[h, :, 0:NP], w=[("qT", hs)])
                        t.dma("sp", kT[hs][:], kT_scr[h, :, 0:NP], w=[("kT", hs)])
                        t.dma("sp", V[hs][:].rearrange("p (j c) -> p j c", c=128), vb_scr[0:NP, hc].rearrange("(j p) c -> p j c", p=128), w=[("V", hs)])
                        t.dma("sp", wbm[hs][:], wb_scr[h], w=[("wbm", hs)])
                        for qc in range(4):
                            qs = qi % 2
                            qi += 1
                            t.dma("sp", sgc[qs][:], sg_scr[h, :, qc * 512:(qc + 1) * 512], w=[("sgc", qs)])
                            nkb = 4 * qc + 4
                            po, pd = 2, 3

                            def pv(i, kb, iis, nkb=nkb, hs=hs):
                                first, last = (i == 0), (i == nkb - 1)
                                t.op("pe", lambda E: E.matmul(PS[po][:, :], lhsT=V[hs][:, kb * 128:(kb + 1) * 128], rhs=Pb[iis][:, :], start=first, stop=last),
                                     r=[("V", hs), ("Pb", iis)], w=[("PS", po)] if first else [], pw=[] if first else [("PS", po)])
                                t.op("pe", lambda E: E.matmul(PS[pd][:, :], lhsT=ones[:, :], rhs=Pb[iis][:, :], start=first, stop=last),
                                     r=["ones", ("Pb", iis)], w=[("PS", pd)] if first else [], pw=[] if first else [("PS", pd)])
                            pend = None
                            for i in range(nkb):
                                kb = i
                                iis = ii % 2
                                ii += 1
                                off = 384 + qc * 512 - kb * 128
                                t.op("pe", lambda E, iis=iis, kb=kb: E.matmul(PS[iis][:, :], lhsT=kT[hs][:, kb * 128:(kb + 1) * 128],
                                                                              rhs=qT[hs][:, qc * 512:(qc + 1) * 512], start=True, stop=True),
                                     r=[("kT", hs), ("qT", hs)], w=[("PS", iis)])
                                t.op("act", lambda E, iis=iis: E.activation(out=Eb[iis][:, :], in_=PS[iis][:, :], func=AF.Exp, scale=SCALE),
                                     r=[("PS", iis)], w=[("Eb", iis)])
                                t.op("dve", lambda E, iis=iis, off=off: E.tensor_tensor(out=Pb[iis][:, :], in0=Eb[iis][:, :], in1=wbm[hs][:, off:off + 512], op=ALU.mult),
                                     r=[("Eb", iis), ("wbm", hs)], w=[("Pb", iis)])
                                if pend is not None:
                                    pv(*pend)
                                pend = (i, kb, iis)
                                yield
                            pv(*pend)
                            t.op("act", lambda E, qs=qs: E.activation(out=lnd[qs][:, :], in_=PS[pd][:, :], func=AF.Ln), r=[("PS", pd)], w=[("lnd", qs)])
                            t.op("act", lambda E, qs=qs: E.activation(out=lnd[qs][:, :], in_=lnd[qs][:, :], func=AF.Exp, scale=-1.0), w=[("lnd", qs)])
                            t.op("dve", lambda E, qs=qs: E.tensor_tensor(out=tmpo[qs][:, :], in0=PS[po][:, :], in1=lnd[qs][:, :], op=ALU.mult),
                                 r=[("PS", po), ("lnd", qs)], w=[("tmpo", qs)])
                            t.op("pool", lambda E, qs=qs: E.tensor_tensor(out=uo[qs][:, :], in0=tmpo[qs][:, :], in1=sgc[qs][:, :], op=ALU.mult),
                                 r=[("tmpo", qs), ("sgc", qs)], w=[("uo", qs)])
                            t.dma("pool", u_scr[h, :, qc * 512:(qc + 1) * 512], uo[qs][:, :], r=[("uo", qs)], pw=["u_scr"], key=("uo", qs))
                            yield

                US = sb(sc, [128, 32 * NS], BF16)

                def g_samp():
                    qs_ = [sb(sc, [128, NS], BF16) for _ in range(2)]
                    kn = [sb(sc, [128, NS], BF16) for _ in range(2)]
                    vn = [[sb(sc, [4, 128], BF16) for _ in range(2)] for _ in range(2)]
                    wbs = [sb(sc, [128, TW], BF16) for _ in range(2)]
                    sgs = [sb(sc, [128, NS], F32) for _ in range(2)]
                    Kc = [sb(sc, [128, 2048], F32) for _ in range(2)]
                    KcT = [sb(sc, [128, 2048], BF16) for _ in range(2)]
                    Vc = [sb(sc, [128, 2048], BF16) for _ in range(2)]
                    Es = [sb(sc, [128, 68], F32) for _ in range(2)]
                    Ps = [sb(sc, [128, 68], BF16) for _ in range(2)]
                    rcs = [sb(sc, [128, 4], F32) for _ in range(2)]
                    tms = [sb(sc, [128, 4], F32) for _ in range(2)]
                    ci = 0
                    for h in range(H):
                        hs = h % 2
                        hc = slice(h * 128, (h + 1) * 128)
                        t.dma("sp", qs_[hs][:], qT_scr[h, :, NP:NT], w=[("qs", hs)])
                        t.dma("sp", kn[hs][:], kT_scr[h, :, NP:NT], w=[("kn", hs)])
                        for db in range(2):
                            t.dma("sp", vn[hs][db][:], vb_scr[NP + db * 4:NP + db * 4 + 4, hc], w=[("vn", hs, db)])
                        t.dma("sp", wbs[hs][:], wb_scr[h], w=[("wbs", hs)])
                        t.dma("sp", sgs[hs][:], sg_scr[h, :, NP:NT], w=[("sgs", hs)])
                        for db in range(2):
                            c = ci % 2
                            ci += 1
                            t.dma("sp", Kc[c][:].rearrange("p (j c) -> p j c", c=128), ck[l, db, :, hc].rearrange("(j p) c -> p j c", p=128), w=[("Kc", c)])
                            t.dma("pool", Vc[c][:].rearrange("p (j c) -> p j c", c=128), cv[l, db, :, hc].rearrange("(j p) c -> p j c", p=128), w=[("Vc", c)])
                            yield
                            for jb in range(4):
                                def tr(E, jb=jb, c=c):
                                    ins = None
                                    for jj in range(4):
                                        blk = jb * 4 + jj
                                        ins = E.transpose(out=PS[5][:, jj * 128:(jj + 1) * 128], in_=Kc[c][:, blk * 128:(blk + 1) * 128], identity=ident[:, :])
                                    return ins
                                t.op("pe", tr, r=[("Kc", c), "ident"], w=[("PS", 5)])
                                t.op("dve", lambda E, jb=jb, c=c: E.tensor_copy(out=KcT[c][:, jb * 512:(jb + 1) * 512], in_=PS[5][:, :]),
                                     r=[("PS", 5)], w=[("KcT", c)] if jb == 0 else [], aw=[] if jb == 0 else [("KcT", c)])
                                yield
                            qcol = slice(db * 4, db * 4 + 4)

                            def smm(E, c=c, qcol=qcol):
                                ins = None
                                for j in range(16):
                                    blk = 15 - j
                                    ins = E.matmul(PS[4][:, j * 4:(j + 1) * 4], lhsT=KcT[c][:, blk * 128:(blk + 1) * 128], rhs=qs_[hs][:, qcol], start=True, stop=True,
                                                   skip_group_check=True)
                                ins = E.matmul(PS[4][0:4, 64:68], lhsT=kn[hs][:, qcol], rhs=qs_[hs][:, qcol], start=True, stop=True, skip_group_check=True)
                                return ins
                            t.op("pe", smm, r=[("KcT", c), ("kn", hs), ("qs", hs)], w=[("PS", 4)])
                            t.op("act", lambda E, c=c: E.activation(out=Es[c][:, 0:64], in_=PS[4][:, 0:64], func=AF.Exp, scale=SCALE), r=[("PS", 4)], w=[("Es", c)])
                            t.op("act", lambda E, c=c: E.activation(out=Es[c][0:4, 64:68], in_=PS[4][0:4, 64:68], func=AF.Exp, scale=SCALE), r=[("PS", 4)], aw=[("Es", c)])
                            t.op("dve", lambda E, c=c: E.tensor_tensor(out=Ps[c][:, 0:64].rearrange("p (j c) -> p j c", c=4), in0=Es[c][:, 0:64].rearrange("p (j c) -> p j c", c=4),
                                                                        in1=wbs[hs][:, 512:2560].rearrange("p (j c) -> p j c", c=128)[:, :, 0:4], op=ALU.mult),
                                 r=[("Es", c), ("wbs", hs)], w=[("Ps", c)])
                            t.op("dve", lambda E, c=c: E.tensor_tensor(out=Ps[c][0:4, 64:68], in0=Es[c][0:4, 64:68], in1=wbs[hs][0:4, 384:388], op=ALU.mult),
                                 r=[("Es", c), ("wbs", hs)], aw=[("Ps", c)])
                            yield

                            def spv(E, c=c, db=db):
                                ins = None
                                for j in range(16):
                                    blk = 15 - j
                                    E.matmul(PS[4][:, 128:132], lhsT=Vc[c][:, blk * 128:(blk + 1) * 128], rhs=Ps[c][:, j * 4:(j + 1) * 4], start=(j == 0), stop=False,
                                             skip_group_check=True)
                                E.matmul(PS[4][:, 128:132], lhsT=vn[hs][db][:, :], rhs=Ps[c][0:4, 64:68], start=False, stop=True, skip_group_check=True)
                                for j in range(16):
                                    E.matmul(PS[4][:, 136:140], lhsT=ones[:, :], rhs=Ps[c][:, j * 4:(j + 1) * 4], start=(j == 0), stop=False, skip_group_check=True)
                                ins = E.matmul(PS[4][:, 136:140], lhsT=ones[0:4, :], rhs=Ps[c][0:4, 64:68], start=False, stop=True, skip_group_check=True)
                                return ins
                            t.op("pe", spv, r=[("Ps", c), ("Vc", c), ("vn", hs, db), "ones"], w=[("PS", 4)])
                            t.op("dve", lambda E, c=c: E.reciprocal(out=rcs[c][:, :], in_=PS[4][:, 136:140]), r=[("PS", 4)], w=[("rcs", c)])
                            t.op("dve", lambda E, c=c: E.tensor_tensor(out=tms[c][:, :], in0=PS[4][:, 128:132], in1=rcs[c][:, :], op=ALU.mult), r=[("PS", 4), ("rcs", c)], w=[("tms", c)])
                            t.op("pool", lambda E, c=c, db=db: E.tensor_tensor(out=US[:, h * NS + db * 4:h * NS + db * 4 + 4], in0=tms[c][:, :],
                                                                                in1=sgs[hs][:, db * 4:db * 4 + 4], op=ALU.mult),
                                 r=[("tms", c), ("sgs", hs)], pw=["US"])
                            yield

                def g_lru():
                    wc = sb(sc, [128, NB * 4], F32)
                    prm = sb(sc, [128, 4 * NB], F32)
                    c1 = sb(sc, [128, NB], F32)
                    c2 = sb(sc, [128, NB], F32)
                    etmp = sb(sc, [128, NB], F32)
                    wab = sb(sc, [128, NB * 128], BF16)
                    wxb = sb(sc, [128, NB * 128], BF16)
                    t.dma("sp", wc[:], wconvT[l].rearrange("p n j -> p (n j)"), w=["wc"])
                    t.dma("sp", prm[:], lrup[l].rearrange("i p n -> p i n"), w=["prm"])
                    t.dma("pool", wab[:].rearrange("p (n e) -> p n e", e=128), wa_t[l].rearrange("n d e -> d n e"), w=["wab"])
                    t.dma("pool", wxb[:].rearrange("p (n e) -> p n e", e=128), wx_t[l].rearrange("n d e -> d n e"), w=["wxb"])
                    t.op("act", lambda E: E.activation(out=etmp[:], in_=prm[:, 3 * NB:4 * NB], func=AF.Exp, scale=-1.0), r=["prm"], w=["etmp"])
                    t.op("act", lambda E: E.activation(out=etmp[:], in_=etmp[:], func=AF.Ln, bias=1.0), w=["etmp"])
                    t.op("dve", lambda E: E.tensor_scalar(out=c1[:], in0=etmp[:], scalar1=-8.0, scalar2=None, op0=ALU.mult), r=["etmp"], w=["c1"])
                    t.op("dve", lambda E: E.tensor_scalar(out=c2[:], in0=etmp[:], scalar1=-16.0, scalar2=None, op0=ALU.mult), r=["etmp"], w=["c2"])
                    nprm = sb(sc, [128, 4 * NB], F32)
                    t.op("dve", lambda E: E.tensor_scalar(out=nprm[:], in0=prm[:], scalar1=-1.0, scalar2=None, op0=ALU.mult), r=["prm"], w=["nprm"])
                    NL = 1024
                    xe = [sb(sc, [128, NL + 3], F32) for _ in range(2)]
                    sgl = [sb(sc, [128, NL], F32) for _ in range(2)]
                    xc = sb(sc, [128, NL], F32)
                    xcb = sb(sc, [128, NL], BF16)
                    rr = sb(sc, [128, NL], F32)
                    ig = sb(sc, [128, NL], F32)
                    aa = sb(sc, [128, NL], F32)
                    sq = sb(sc, [128, NL], F32)
                    hh = [sb(sc, [128, NL], F32) for _ in range(2)]
                    ul = [sb(sc, [128, NL], BF16) for _ in range(2)]
                    h0 = sb(sc, [128, 2], F32)
                    ci = 0
                    for n in range(NB):
                        items = [("p", c, NL) for c in range(2)] + [("s", db, 4) for db in range(2)]
                        prev_h = None
                        for (knd, idx, N) in items:
                            s = ci % 2
                            ci += 1
                            if knd == "p":
                                c0 = idx * NL
                                if idx == 0:
                                    t.op("dve", lambda E, s=s: E.memset(xe[s][:, 0:3], 0.0), w=[("xe", s)])
                                    t.dma("sp", xe[s][:, 3:NL + 3], xl_scr[n, :, 0:NL], aw=[("xe", s)], key=("xe", s))
                                else:
                                    t.dma("sp", xe[s][:, 0:NL + 3], xl_scr[n, :, c0 - 3:c0 + NL], w=[("xe", s)])
                                t.dma("sp", sgl[s][:, 0:N], sgl_scr[n, :, c0:c0 + N], w=[("sgl", s)])
                                init = 0.0 if idx == 0 else prev_h
                            else:
                                db = idx
                                t.dma("sp", xe[s][:, 0:3], sconvT[l, db, n * 128:(n + 1) * 128, :], w=[("xe", s)])
                                t.dma("sp", xe[s][:, 3:7], xl_scr[n, :, NP + db * 4:NP + db * 4 + 4], aw=[("xe", s)], key=("xe", s))
                                t.dma("sp", sgl[s][:, 0:N], sgl_scr[n, :, NP + db * 4:NP + db * 4 + 4], w=[("sgl", s)])
                                t.dma("sp", h0[:, db:db + 1], sh[l, db, n * 128:(n + 1) * 128].rearrange("(p o) -> p o", o=1), w=[("h0", db)])
                                init = h0[:, db:db + 1]
                            w4 = [wc[:, n * 4 + j:n * 4 + j + 1] for j in range(4)]
                            t.op("dve", lambda E, s=s, N=N, w4=w4: E.tensor_scalar(out=xc[:, 0:N], in0=xe[s][:, 3:3 + N], scalar1=w4[3], scalar2=prm[:, n:n + 1],
                                                                                   op0=ALU.mult, op1=ALU.add), r=[("xe", s), "wc", "prm"], w=["xc"])
                            for j in range(3):
                                t.op("dve", lambda E, s=s, N=N, j=j, w4=w4: E.scalar_tensor_tensor(out=xc[:, 0:N], in0=xe[s][:, j:j + N], scalar=w4[j], in1=xc[:, 0:N],
                                                                                                    op0=ALU.mult, op1=ALU.add), r=[("xe", s)], w=["xc"])
                            yield
                            t.op("pool", lambda E, N=N: E.tensor_copy(out=xcb[:, 0:N], in_=xc[:, 0:N]), r=["xc"], w=["xcb"])
                            yield
                            chunks = [(c0_, min(512, N - c0_)) for c0_ in range(0, N, 512)]
                            k2 = 0
                            for (wmat, dst, bofs, key) in ((wab, rr, NB, "rr"), (wxb, ig, 2 * NB, "ig")):
                                for (cc0, cn) in chunks:
                                    pbk = 6 + (k2 % 2)
                                    k2 += 1
                                    t.op("pe", lambda E, wmat=wmat, cc0=cc0, cn=cn, pbk=pbk: E.matmul(PS[pbk][:, 0:cn], lhsT=wmat[:, n * 128:(n + 1) * 128],
                                                                                                      rhs=xcb[:, cc0:cc0 + cn], start=True, stop=True),
                                         r=["wab", "wxb", "xcb"], w=[("PS", pbk)])
                                    fst = (cc0 == 0)
                                    t.op("act", lambda E, dst=dst, cc0=cc0, cn=cn, pbk=pbk, bofs=bofs: E.activation(
                                        out=dst[:, cc0:cc0 + cn], in_=PS[pbk][:, 0:cn], func=AF.Exp, scale=-1.0, bias=nprm[:, bofs + n:bofs + n + 1]),
                                        r=[("PS", pbk), "nprm"], w=[key] if fst else [], aw=[] if fst else [key])
                                    yield
                                t.op("act", lambda E, dst=dst, N=N: E.activation(out=dst[:, 0:N], in_=dst[:, 0:N], func=AF.Ln, bias=1.0), w=[key])
                                t.op("act", lambda E, dst=dst, N=N: E.activation(out=dst[:, 0:N], in_=dst[:, 0:N], func=AF.Exp, scale=-1.0), w=[key])
                                yield
                            t.op("act", lambda E, N=N: E.activation(out=aa[:, 0:N], in_=rr[:, 0:N], func=AF.Exp, scale=c1[:, n:n + 1]), r=["rr", "c1"], w=["aa"])
                            t.op("act", lambda E, N=N: E.activation(out=sq[:, 0:N], in_=rr[:, 0:N], func=AF.Exp, scale=c2[:, n:n + 1]), r=["rr", "c2"], w=["sq"])
                            t.op("act", lambda E, N=N: E.activation(out=sq[:, 0:N], in_=sq[:, 0:N], func=AF.Ln, scale=-1.0, bias=1.0), w=["sq"])
                            t.op("act", lambda E, N=N: E.activation(out=sq[:, 0:N], in_=sq[:, 0:N], func=AF.Exp, scale=0.5), w=["sq"])
                            yield
                            t.op("pool", lambda E, N=N: E.tensor_tensor(out=ig[:, 0:N], in0=ig[:, 0:N], in1=xc[:, 0:N], op=ALU.mult), r=["xc"], w=["ig"])
                            t.op("pool", lambda E, N=N: E.tensor_tensor(out=ig[:, 0:N], in0=ig[:, 0:N], in1=sq[:, 0:N], op=ALU.mult), r=["sq"], w=["ig"])
                            yield
                            rds = ["aa", "ig"]
                            if knd == "p" and idx > 0:
                                rds.append(("hh", 1 - s))
                            if knd == "s":
                                rds.append(("h0", idx))
                            t.op("dve", lambda E, s=s, N=N, init=init: E.tensor_tensor_scan(out=hh[s][:, 0:N], data0=aa[:, 0:N], data1=ig[:, 0:N], initial=init,
                                                                                             op0=ALU.mult, op1=ALU.add), r=rds, w=[("hh", s)])
                            prev_h = hh[s][:, N - 1:N]
                            yield
                            if knd == "p":
                                t.op("pool", lambda E, s=s, N=N: E.tensor_tensor(out=ul[s][:, 0:N], in0=hh[s][:, 0:N], in1=sgl[s][:, 0:N], op=ALU.mult),
                                     r=[("hh", s), ("sgl", s)], w=[("ul", s)])
                                t.dma("pool", u_scr[16 + n, :, c0:c0 + N], ul[s][:, 0:N], r=[("ul", s)], pw=["u_scr"], key=("ul", s))
                                if idx == 1:
                                    t.dma("pool", hp[l, 0, n * 128:(n + 1) * 128].rearrange("(p o) -> p o", o=1), hh[s][:, NL - 1:NL], r=[("hh", s)], pw=["hp"], key=("hho", s))
                                    t.dma("pool", convT[l, 0, n * 128:(n + 1) * 128, :], xe[s][:, NL:NL + 3], r=[("xe", s)], pw=["convT"], key=("xeo", s))
                            else:
                                t.op("pool", lambda E, s=s, N=N, idx=idx: E.tensor_tensor(out=US[:, (16 + n) * NS + idx * 4:(16 + n) * NS + idx * 4 + 4], in0=hh[s][:, 0:N],
                                                                                          in1=sgl[s][:, 0:N], op=ALU.mult),
                                     r=[("hh", s), ("sgl", s)], pw=["US"])
                                t.dma("pool", hp[l, 1 + idx, n * 128:(n + 1) * 128].rearrange("(p o) -> p o", o=1), hh[s][:, 3:4], r=[("hh", s)], pw=["hp"], key=("hho", s))
                                t.dma("pool", convT[l, 1 + idx, n * 128:(n + 1) * 128, :], xe[s][:, 4:7], r=[("xe", s)], pw=["convT"], key=("xeo", s))
                            yield

                gens = [g_attn(), g_samp(), g_lru()]
                if do_ada_next:
                    gens.append(ada_gen(l + 1, sc, 5))
                run_gens(gens)
                t.dma("pool", u_scr[:, :, NP:NT].rearrange("k p t -> p k t"), US[:].rearrange("p (k t) -> p k t", t=NS), r=["US"], pw=["u_scr"], key="USo")
                t.barrier()

        def phase_e(l, xsrc, xdst):
            with contextlib.ExitStack() as sc:
                GGp = sb(sc, [128, D], F32)
                GGs = sb(sc, [NS, D], F32)
                gpb = sb(sc, [128, D], F32)
                yt = [sb(sc, [128, D], F32) for _ in range(2)]
                xt = [sb(sc, [128, D], F32) for _ in range(2)]
                junk = sb(sc, [128, D], BF16)
                stat = sb(sc, [128, 17 * 4], F32)
                t.dma("sp", GGp[:], bass.AP(mod_h, (l * 3 + 0) * D, [[0, 128], [1, D]]), w=["GGp"])
                t.dma("sp", GGs[0:4, :], bass.AP(mod_h, (l * 3 + 1) * D, [[0, 4], [1, D]]), w=["GGs"])
                t.dma("sp", GGs[4:8, :], bass.AP(mod_h, (l * 3 + 2) * D, [[0, 4], [1, D]]), aw=["GGs"], key="GGs")
                t.dma("sp", gpb[:], bass.AP(gpost_h, l * D, [[0, 128], [1, D]]), w=["gpb"])
                t.op("dve", lambda E: E.tensor_tensor(out=GGp[:], in0=GGp[:], in1=gpb[:], op=ALU.mult), r=["gpb"], w=["GGp"])
                t.op("dve", lambda E: E.tensor_tensor(out=GGs[:], in0=GGs[:], in1=gpb[0:NS, :], op=ALU.mult), r=["gpb"], w=["GGs"])
                for tt in range(17):
                    M = 128 if tt < 16 else NS
                    r0 = tt * 128
                    s = tt % 2
                    so = tt * 4
                    GG = GGp if tt < 16 else GGs
                    gk = "GGp" if tt < 16 else "GGs"
                    t.dma("sp", yt[s][0:M, :], y_scr[r0:r0 + M, :], w=[("yt", s)])
                    t.dma("sp", xt[s][0:M, :], xsrc[r0:r0 + M, :], w=[("xt", s)])
                    t.op("act", lambda E, s=s, M=M, so=so: E.activation(out=junk[0:M, :], in_=yt[s][0:M, :], func=AF.Square, accum_out=stat[0:M, so:so + 1]),
                         r=[("yt", s)], w=["junk", ("st", 0)])
                    t.op("dve", lambda E, M=M, so=so: E.tensor_scalar(out=stat[0:M, so + 1:so + 2], in0=stat[0:M, so:so + 1], scalar1=1.0 / D, scalar2=EPS,
                                                                       op0=ALU.mult, op1=ALU.add), r=[("st", 0)], w=[("st", 1)])
                    t.op("act", lambda E, M=M, so=so: E.activation(out=stat[0:M, so + 2:so + 3], in_=stat[0:M, so + 1:so + 2], func=AF.Sqrt),
                         r=[("st", 1)], w=[("st", 2)])
                    t.op("dve", lambda E, M=M, so=so: E.reciprocal(out=stat[0:M, so + 3:so + 4], in_=stat[0:M, so + 2:so + 3]), r=[("st", 2)], w=[("st", 3)])
                    t.op("dve", lambda E, s=s, M=M, so=so, GG=GG: E.scalar_tensor_tensor(out=yt[s][0:M, :], in0=yt[s][0:M, :], scalar=stat[0:M, so + 3:so + 4],
                                                                                         in1=GG[0:M, :], op0=ALU.mult, op1=ALU.mult),
                         r=[("st", 3), gk], w=[("yt", s)])
                    t.op("pool", lambda E, s=s, M=M: E.tensor_tensor(out=xt[s][0:M, :], in0=xt[s][0:M, :], in1=yt[s][0:M, :], op=ALU.add),
                         r=[("yt", s)], w=[("xt", s)])
                    t.dma("pool", xdst[r0:r0 + M, :], xt[s][0:M, :], r=[("xt", s)], pw=["xdst"], key=("xto", s))

        for l in range(DEPTH):
            xsrc = x_in if l == 0 else x1_scr
            xdst = x1_scr if l == 0 else yp
            with contextlib.ExitStack() as sc:
                BIG = sb(sc, [128, 32 * NT], BF16, "BIGh")
                with nc.named_scope(f"A{l}"):
                    phase_a(l, xsrc, BIG)
                    t.barrier()
                if stop <= 2:
                    return nc
                with nc.named_scope(f"B{l}"):
                    proj(l, win[l], NGI, kind_in, BIG)
                    t.barrier()
            if stop <= 3:
                return nc
            with nc.named_scope(f"C{l}"):
                phase_c(l, l + 1 < DEPTH)
            if stop <= 4:
                return nc
            with contextlib.ExitStack() as sc:
                BIG = sb(sc, [128, 32 * NT], BF16, "BIGu")
                with nc.named_scope(f"D{l}"):
                    t.dma("sp", BIG[:].rearrange("p (k t) -> p k t", t=NT), u_scr[:, :, 0:NT].rearrange("k p t -> p k t"), w=["BIG"])
                    proj(l, wout[l], NGO, lambda cbi: "y", BIG)
                    t.barrier()
            if stop <= 5:
                return nc
            with nc.named_scope(f"E{l}"):
                phase_e(l, xsrc, xdst)
                t.barrier()
            if stop <= 6:
                return nc
    return nc


def _bucket(i):
    if i < 16:
        return i
    d = np.float32(i)
    v = np.float32(16) + np.log(np.maximum(d, np.float32(1.0)) / np.float32(16)) / np.float32(math.log(2048 / 16)) * np.float32(16)
    return int(min(int(np.float32(v)), 31))


def _consts():
    cp = np.zeros((32, TVL), np.float32)
    for i in range(0, 2049):
        b = _bucket(i)
        cnt = 0
        for (wdw, dil) in ((128, 1), (512, 4), (2048, 16)):
            if i % dil == 0 and i // dil <= wdw // dil:
                cnt += 1
        if cnt:
            cp[b, i + TOFF] += cnt
    jm = np.eye(128, dtype=np.float32)[::-1].copy()
    return cp, jm, np.eye(128, dtype=np.float32)


_NC_CACHE = {}
_NCORES = [8]
_STOP = [99]
_RUNKW = {}
_DBG = {'ntt': 17, 'evac': True}


def kernel(x_prompt, x_sample, cache_k, cache_v, state_h, state_conv, c_prompt, c_sample, rel_table, w_ada, b_ada,
           g_pre, w_in, w_conv, b_conv, w_a, b_a, w_x, b_x, lam, w_out, g_post):
    f = lambda a: np.ascontiguousarray(np.asarray(a, dtype=np.float32))
    x_prompt, x_sample, cache_k, cache_v = f(x_prompt), f(x_sample), f(cache_k), f(cache_v)
    state_h, state_conv, c_prompt, c_sample = f(state_h), f(state_conv), f(c_prompt), f(c_sample)
    cp, jm, idn = _consts()

    def tile_w(w, ng):
        L = w.shape[0]
        return np.ascontiguousarray(f(w).reshape(L, 32, 128, ng, 256).transpose(0, 3, 2, 1, 4)).reshape(L, ng, 128, 8192)
    shared = {
        "rel": f(rel_table), "cpad": cp, "jmat": jm, "ident": idn,
        "wada": tile_w(w_ada, NGI), "bada": f(b_ada),
        "gpreT": np.ascontiguousarray(f(g_pre).reshape(DEPTH, 32, 128).transpose(0, 2, 1)),
        "win": tile_w(w_in, NGI), "wout": tile_w(w_out, NGO),
        "wconvT": np.ascontiguousarray(f(w_conv).reshape(DEPTH, 4, NB, 128).transpose(0, 3, 2, 1)),
        "lrup": np.ascontiguousarray(np.stack([f(b_conv), f(b_a), f(b_x), f(lam)], axis=1).reshape(DEPTH, 4, NB, 128).transpose(0, 1, 3, 2)),
        "wa_t": f(w_a), "wx_t": f(w_x), "gpost": f(g_post),
    }
    in_maps = []
    for c in range(_NCORES[0]):
        b = c % 4
        m = dict(shared)
        m["x_in"] = np.ascontiguousarray(np.concatenate([x_prompt[b], x_sample[2 * b], x_sample[2 * b + 1]], axis=0))
        m["c3"] = np.ascontiguousarray(np.stack([c_prompt[b], c_sample[2 * b], c_sample[2 * b + 1]], axis=0))
        m["ck"] = np.ascontiguousarray(cache_k[:, 2 * b:2 * b + 2].reshape(DEPTH, 2, NP, H * 128))
        m["cv"] = np.ascontiguousarray(cache_v[:, 2 * b:2 * b + 2].reshape(DEPTH, 2, NP, H * 128))
        m["sh"] = np.ascontiguousarray(state_h[:, 2 * b:2 * b + 2])
        m["sconvT"] = np.ascontiguousarray(state_conv[:, 2 * b:2 * b + 2].transpose(0, 1, 3, 2))
        in_maps.append(m)
    if "nc" not in _NC_CACHE:
        _NC_CACHE["nc"] = build_nc(_STOP[0])
    res = run_bass_kernel_spmd(_NC_CACHE["nc"], in_maps, core_ids=list(range(_NCORES[0])), **_RUNKW)
    _DBG["res"] = res
    R = res.results
    if _NCORES[0] < 8:
        R = [R[c % _NCORES[0]] for c in range(8)]
    B, DB = 4, 8
    y_p = np.zeros((B, NP, D), np.float32)
    y_s = np.zeros((DB, 4, D), np.float32)
    k_p = np.zeros((DEPTH, B, NP, H, 128), np.float32)
    v_p = np.zeros_like(k_p)
    h_p = np.zeros((DEPTH, B, 2048), np.float32)
    c_p = np.zeros((DEPTH, B, 3, 2048), np.float32)
    k_s = np.zeros((DEPTH, DB, 4, H, 128), np.float32)
    v_s = np.zeros_like(k_s)
    h_s = np.zeros((DEPTH, DB, 2048), np.float32)
    c_s = np.zeros((DEPTH, DB, 3, 2048), np.float32)
    for b in range(B):
        r = R[b]
        y_p[b] = r["yp"][:NP]
        kpr, vpr, hpr, cvr = r["kp"], r["vp"], r["hp"], r["convT"]
        k_p[:, b] = kpr[:, :NP].reshape(DEPTH, NP, H, 128)
        v_p[:, b] = vpr[:, :NP].reshape(DEPTH, NP, H, 128)
        h_p[:, b] = hpr[:, 0]
        c_p[:, b] = cvr[:, 0].transpose(0, 2, 1)
        for db in range(2):
            y_s[2 * b + db] = r["yp"][NP + 4 * db:NP + 4 * db + 4]
            k_s[:, 2 * b + db] = kpr[:, NP + 4 * db:NP + 4 * db + 4].reshape(DEPTH, 4, H, 128)
            v_s[:, 2 * b + db] = vpr[:, NP + 4 * db:NP + 4 * db + 4].reshape(DEPTH, 4, H, 128)
            h_s[:, 2 * b + db] = hpr[:, 1 + db]
            c_s[:, 2 * b + db] = cvr[:, 1 + db].transpose(0, 2, 1)
    return (y_p, y_s, k_p, v_p, h_p, c_p, k_s, v_s, h_s, c_s)
```

```python
import contextlib
import math
import numpy as np
import concourse.bass as bass
import concourse.mybir as mybir
from concourse.bass_utils import run_bass_kernel_spmd

F32, BF16 = mybir.dt.float32, mybir.dt.bfloat16
AF = mybir.ActivationFunctionType
ALU = mybir.AluOpType

D = 4096
NP = 2048
NS = 8
NT = NP + NS
NTP = 2080
H = 16
NB = 16
EPS = 1e-6
SCALE = 128 ** -0.5
TW = 2560
TVL = 2688
TOFF = 511
DEPTH = 2
NGI = 48
NGO = 16


class Sem:
    pass


class Trk:
    def __init__(self, nc, es):
        self.nc, self.es = nc, es
        self.eng = {"pe": nc.tensor, "act": nc.scalar, "dve": nc.vector, "pool": nc.gpsimd, "sp": nc.sync}
        self.selfsem = {}
        self.allsems = []
        self.W, self.R = {}, {}
        self.seen = {k: {} for k in self.eng}
        self.dmasem = {}
        self.cnt = 0
        self.epoch()

    def newsem(self, name):
        s = Sem()
        self.cnt += 1
        s.h = self.es.enter_context(self.nc.semaphore(f"{name}{self.cnt}"))
        s.n = 0
        self.allsems.append(s)
        return s

    def epoch(self):
        for k in self.eng:
            self.selfsem[k] = self.newsem("e" + k)

    def _wait(self, e, sem, val):
        if val <= 0 or self.seen[e].get(sem, 0) >= val:
            return
        self.eng[e].wait_ge(sem.h, val)
        self.seen[e][sem] = val

    def op(self, e, fn, r=(), w=(), pw=(), aw=(), sem=None, k=1):
        waits = {}

        def add(d):
            for s, v in d.items():
                if waits.get(s, 0) < v:
                    waits[s] = v
        for b in r:
            add(self.W.get(b, {}))
            if isinstance(b, tuple) and b[0] == "PS":
                for s_, v_ in self.R.get(b, {}).items():
                    if s_ is not self.selfsem[e] and waits.get(s_, 0) < v_:
                        waits[s_] = v_
        for b in list(w) + list(aw):
            add(self.W.get(b, {}))
            add(self.R.get(b, {}))
        for s, v in waits.items():
            self._wait(e, s, v)
        ins = fn(self.eng[e])
        if sem is None:
            sem = self.selfsem[e]
        ins.then_inc(sem.h, k)
        sem.n += k
        for b in w:
            self.W[b] = {sem: sem.n}
            self.R[b] = {}
        for b in list(pw) + list(aw):
            self.W.setdefault(b, {})[sem] = sem.n
        for b in r:
            self.R.setdefault(b, {})[sem] = sem.n
        return (sem, sem.n)

    def dma(self, q, out, in_, r=(), w=(), pw=(), aw=(), key=None, **kw):
        if key is None:
            key = w[0] if w else (aw[0] if aw else (pw[0] if pw else r[0]))
        ds = self.dmasem.get(key)
        if ds is None:
            ds = self.dmasem[key] = self.newsem("d")
        return self.op(q, lambda E: E.dma_start(out=out, in_=in_, **kw), r=r, w=w, pw=pw, aw=aw, sem=ds, k=16)

    def barrier(self):
        for e in self.eng:
            for s in self.allsems:
                self._wait(e, s, s.n)
        self.W.clear()
        self.R.clear()


def build_nc(stop=99):
    nc = bass.Bass("TRN2", target_bir_lowering=False)

    def din(name, shape, dt=F32):
        return nc.dram_tensor(name, shape, dt, kind="ExternalInput")

    def dout(name, shape, dt=F32):
        return nc.dram_tensor(name, shape, dt, kind="ExternalOutput")

    def dscr(name, shape, dt=F32):
        return nc.dram_tensor(name, shape, dt, kind="Internal")

    x_in = din("x_in", [NT, D]).ap()
    c3 = din("c3", [3, D]).ap()
    ck = din("ck", [DEPTH, 2, NP, H * 128]).ap()
    cv = din("cv", [DEPTH, 2, NP, H * 128]).ap()
    sh = din("sh", [DEPTH, 2, 2048]).ap()
    sconvT = din("sconvT", [DEPTH, 2, 2048, 3]).ap()
    rel = din("rel", [32, 16]).ap()
    cpad = din("cpad", [32, TVL]).ap()
    jmat_d = din("jmat", [128, 128]).ap()
    ident_d = din("ident", [128, 128]).ap()
    wada = din("wada", [DEPTH, NGI, 128, 8192]).ap()
    bada_h = din("bada", [DEPTH, 12288])
    gpreT = din("gpreT", [DEPTH, 128, 32]).ap()
    win = din("win", [DEPTH, NGI, 128, 8192]).ap()
    wout = din("wout", [DEPTH, NGO, 128, 8192]).ap()
    wconvT = din("wconvT", [DEPTH, 128, NB, 4]).ap()
    lrup = din("lrup", [DEPTH, 4, 128, NB]).ap()
    wa_t = din("wa_t", [DEPTH, NB, 128, 128]).ap()
    wx_t = din("wx_t", [DEPTH, NB, 128, 128]).ap()
    gpost_h = din("gpost", [DEPTH, D])

    yp = dout("yp", [NT, D]).ap()
    kp = dout("kp", [DEPTH, NT, H * 128]).ap()
    vp = dout("vp", [DEPTH, NT, H * 128]).ap()
    hp = dout("hp", [DEPTH, 3, 2048]).ap()
    convT = dout("convT", [DEPTH, 3, 2048, 3]).ap()

    qT_scr = dscr("qT_scr", [H, 128, NTP], BF16).ap()
    kT_scr = dscr("kT_scr", [H, 128, NTP], BF16).ap()
    vb_scr = dscr("vb_scr", [NT, H * 128], BF16).ap()
    sg_scr = dscr("sg_scr", [H, 128, NTP]).ap()
    xl_scr = dscr("xl_scr", [NB, 128, NTP]).ap()
    sgl_scr = dscr("sgl_scr", [NB, 128, NTP]).ap()
    y_scr = dscr("y_scr", [NT, D]).ap()
    x1_scr = dscr("x1_scr", [NT, D]).ap()
    mod_h = dscr("mod_scr", [DEPTH, 3, D])
    mod_scr = mod_h.ap()
    tv_h = dscr("tv_scr", [H, TVL])
    tv_scr = tv_h.ap()
    wb_scr = dscr("wb_scr", [H, 128, TW], BF16).ap()
    u_scr = dscr("u_scr", [32, 128, NTP], BF16).ap()

    uid = [0]

    def sb(scope, shape, dt, name="t"):
        uid[0] += 1
        return scope.enter_context(nc.sbuf_tensor(f"{name}{uid[0]}", shape, dt))

    with contextlib.ExitStack() as es:
        t = Trk(nc, es)
        PS = [es.enter_context(nc.psum_tensor(f"ps{i}", [128, 512], F32)) for i in range(8)]
        ident = sb(es, [128, 128], F32, "ident")
        jmat = sb(es, [128, 128], F32, "jmat")
        ones = sb(es, [128, 128], BF16, "ones")
        AB = sb(es, [128, DEPTH * 2 * 32 * 3], F32, "AB")
        scT = sb(es, [128, 96], BF16, "scT")

        def ABap(l, ab, kt, g):
            o = ((l * 2 + ab) * 32 + kt) * 3 + g
            return AB[:, o:o + 1]

        t.dma("sp", ident[:], ident_d, w=["ident"])
        t.dma("sp", jmat[:], jmat_d, w=["jmat"])
        t.op("dve", lambda E: E.memset(ones[:], 1.0), w=["ones"])

        with contextlib.ExitStack() as sc:
            rel_sb = sb(sc, [32, 16], F32)
            cp_sb = sb(sc, [32, TVL], F32)
            et = sb(sc, [32, 16], F32)
            tv_sb = sb(sc, [16, TVL], F32)
            hk = [sb(sc, [128, TW], F32) for _ in range(2)]
            wbm = [sb(sc, [128, TW], BF16) for _ in range(2)]
            t.dma("sp", rel_sb[:], rel, w=["rel"])
            t.dma("sp", cp_sb[:], cpad, w=["cp"])
            t.op("act", lambda E: E.activation(out=et[:], in_=rel_sb[:], func=AF.Exp), r=["rel"], w=["et"])
            for ch in range(6):
                n = min(512, TVL - ch * 512)
                t.op("pe", lambda E, ch=ch, n=n: E.matmul(PS[ch][0:16, 0:n], lhsT=et[:], rhs=cp_sb[:, ch * 512:ch * 512 + n],
                                                           start=True, stop=True), r=["et", "cp"], w=[("PS", ch)])
                t.op("dve", lambda E, ch=ch, n=n: E.tensor_copy(out=tv_sb[:, ch * 512:ch * 512 + n], in_=PS[ch][0:16, 0:n]),
                     r=[("PS", ch)], pw=["tv"])
            t.dma("sp", tv_scr, tv_sb[:], r=["tv"], w=["tv_scr"])
            bk = 0
            for h in range(H):
                s = h % 2
                src = bass.AP(tv_h, h * TVL, [[1, 128], [1, TW]])
                t.dma("sp", hk[s][:], src, r=["tv_scr"], w=[("hk", s)])
                for ch in range(5):
                    b = bk % 8
                    bk += 1
                    t.op("pe", lambda E, s=s, ch=ch, b=b: E.matmul(PS[b][:, :], lhsT=jmat[:], rhs=hk[s][:, ch * 512:(ch + 1) * 512],
                                                                    start=True, stop=True), r=["jmat", ("hk", s)], w=[("PS", b)])
                    first = (ch == 0)
                    if ch % 2 == 0:
                        t.op("act", lambda E, s=s, ch=ch, b=b: E.activation(out=wbm[s][:, ch * 512:(ch + 1) * 512], in_=PS[b][:, :], func=AF.Copy),
                             r=[("PS", b)], w=[("wbm", s)] if first else [], aw=[] if first else [("wbm", s)])
                    else:
                        t.op("dve", lambda E, s=s, ch=ch, b=b: E.tensor_copy(out=wbm[s][:, ch * 512:(ch + 1) * 512], in_=PS[b][:, :]),
                             r=[("PS", b)], aw=[("wbm", s)])
                t.dma("sp", wb_scr[h], wbm[s][:], r=[("wbm", s)], pw=["wb_scr"])
        t.barrier()
        if stop <= 0:
            return nc

        def ada_prep():
            with contextlib.ExitStack() as sc:
                c3s = sb(sc, [3, D], F32)
                t.dma("sp", c3s[:], c3, w=["c3s"])
                t.op("act", lambda E: E.activation(out=c3s[:], in_=c3s[:], func=AF.Silu), w=["c3s"])
                for kt in range(32):
                    t.op("pe", lambda E, kt=kt: E.transpose(out=PS[0][:, kt * 3:(kt + 1) * 3], in_=c3s[:, kt * 128:(kt + 1) * 128],
                                                            identity=ident[0:3, 0:3]), r=["c3s", "ident"],
                         w=[("PS", 0)] if kt == 0 else [], pw=[] if kt == 0 else [("PS", 0)])
                t.op("dve", lambda E: E.tensor_copy(out=scT[:], in_=PS[0][:, 0:96]), r=[("PS", 0)], w=["scT"])
                t.barrier()

        def ada_tiles(sc):
            return ([sb(sc, [128, 8192], BF16) for _ in range(2)], [sb(sc, [3, 256], F32) for _ in range(2)],
                    [sb(sc, [3, 256], F32) for _ in range(2)], sb(sc, [128, 192], F32), sb(sc, [128, 32], F32), sb(sc, [128, 96], F32))

        def ada_gen(l, tiles, pbank, groups):
            Wb, bb, modg, modT, gpr, tmp1 = tiles
            def ada_load(g):
                s_ = g % 2
                t.dma("pool", Wb[s_][:], wada[l, g], w=[("adaW", s_)], max_dma_last_dim=8192)
                t.dma("sp", bb[s_][:], bass.AP(bada_h, l * 12288 + g * 256, [[0, 3], [1, 256]]), w=[("bb", s_)])
            ada_load(groups[0])
            for gi_, g in enumerate(groups):
                s = g % 2
                pb = pbank
                if gi_ + 1 < len(groups):
                    ada_load(groups[gi_ + 1])

                def mm(E, s=s, pb=pb):
                    ins = None
                    for kt in range(32):
                        ins = E.matmul(PS[pb][0:3, 0:256], lhsT=scT[:, kt * 3:(kt + 1) * 3], rhs=Wb[s][:, kt * 256:(kt + 1) * 256],
                                       start=(kt == 0), stop=(kt == 31))
                    return ins
                t.op("pe", mm, r=[("adaW", s), "scT"], w=[("PS", pb)])
                t.op("dve", lambda E, s=s, pb=pb: E.tensor_tensor(out=modg[s][:], in0=PS[pb][0:3, 0:256], in1=bb[s][:], op=ALU.add),
                     r=[("PS", pb), ("bb", s)], w=[("modg", s)])
                if g < 32:
                    def tr(E, s=s, pb=pb):
                        ins = None
                        for cb in range(2):
                            ins = E.transpose(out=PS[pb][:, 256 + cb * 3:259 + cb * 3], in_=modg[s][:, cb * 128:(cb + 1) * 128], identity=ident[0:3, 0:3])
                        return ins
                    t.op("pe", tr, r=[("modg", s), "ident"], w=[("PS", pb)])
                    t.op("dve", lambda E, g=g, pb=pb: E.tensor_copy(out=modT[:, g * 6:g * 6 + 6], in_=PS[pb][:, 256:262]), r=[("PS", pb)], pw=["modT"])
                else:
                    t.dma("sp", mod_scr[l, :, (g - 32) * 256:(g - 31) * 256], modg[s][:], r=[("modg", s)], pw=["mod_scr"])
                for _ in range(10):
                    yield
            if groups[0] >= 32:
                return
            t.dma("sp", gpr[:], gpreT[l], w=["gpr"])
            t.op("dve", lambda E: E.tensor_scalar(out=tmp1[:], in0=modT[:, 96:192], scalar1=1.0, scalar2=None, op0=ALU.add),
                 r=["modT"], w=["tmp1"])
            oa = (l * 2 + 0) * 96
            ob = (l * 2 + 1) * 96
            for g3 in range(3):
                t.op("dve", lambda E, g3=g3: E.tensor_tensor(out=AB[:, oa:oa + 96].rearrange("p (k g) -> p k g", g=3)[:, :, g3],
                                                              in0=tmp1[:].rearrange("p (k g) -> p k g", g=3)[:, :, g3], in1=gpr[:], op=ALU.mult),
                     r=["tmp1", "gpr"], pw=["AB"])
            t.op("dve", lambda E: E.tensor_copy(out=AB[:, ob:ob + 96], in_=modT[:, 0:96]), r=["modT"], pw=["AB"])
            yield

        def run_gens(gens):
            alive = list(gens)
            while alive:
                for g in list(alive):
                    try:
                        next(g)
                    except StopIteration:
                        alive.remove(g)

        ada_prep()
        with contextlib.ExitStack() as sc:
            run_gens([ada_gen(0, ada_tiles(sc), 1, list(range(NGI)))])
        t.barrier()
        if stop <= 1:
            return nc

        def phase_a(l, xsrc, BIG):
            with contextlib.ExitStack() as sc:
                xa = [sb(sc, [128, D], F32) for _ in range(2)]
                junk = sb(sc, [128, D], BF16)
                stat = sb(sc, [128, 17 * 4], F32)
                diag = [sb(sc, [128, 128], F32) for _ in range(2)]
                ei = 0
                bi = 0
                for tt in range(_DBG['ntt']):
                    M = 128 if tt < 16 else NS
                    r0 = tt * 128
                    s = tt % 2
                    so = tt * 4
                    t.dma("sp", xa[s][0:M, :], xsrc[r0:r0 + M, :], w=[("xa", s)])
                    t.op("act", lambda E, s=s, M=M, so=so: E.activation(out=junk[0:M, :], in_=xa[s][0:M, :], func=AF.Square,
                                                                         accum_out=stat[0:M, so:so + 1]),
                         r=[("xa", s)], w=["junk", ("st", 0)])
                    t.op("dve", lambda E, M=M, so=so: E.tensor_scalar(out=stat[0:M, so + 1:so + 2], in0=stat[0:M, so:so + 1], scalar1=1.0 / D,
                                                                       scalar2=EPS, op0=ALU.mult, op1=ALU.add), r=[("st", 0)], w=[("st", 1)])
                    t.op("act", lambda E, M=M, so=so: E.activation(out=stat[0:M, so + 2:so + 3], in_=stat[0:M, so + 1:so + 2], func=AF.Sqrt),
                         r=[("st", 1)], w=[("st", 2)])
                    t.op("dve", lambda E, M=M, so=so: E.reciprocal(out=stat[0:M, so + 3:so + 4], in_=stat[0:M, so + 2:so + 3]),
                         r=[("st", 2)], w=[("st", 3)])
                    t.op("dve", lambda E, s=s, M=M, so=so: E.tensor_scalar(out=xa[s][0:M, :], in0=xa[s][0:M, :], scalar1=stat[0:M, so + 3:so + 4],
                                                                            scalar2=None, op0=ALU.mult), r=[("st", 3)], w=[("xa", s)])
                    for q4 in range(8):
                        b = bi % 6
                        bi += 1

                        def mm(E, s=s, M=M, q4=q4, b=b):
                            ins = None
                            for j in range(4):
                                kt = q4 * 4 + j
                                ins = E.transpose(out=PS[b][:, j * 128:j * 128 + M], in_=xa[s][0:M, kt * 128:(kt + 1) * 128], identity=ident[0:M, 0:M])
                            return ins
                        t.op("pe", mm, r=[("xa", s), "ident"], w=[("PS", b)])
                        for j in range(4):
                            kt = q4 * 4 + j
                            grps = [(0, M, 0)] if tt < 16 else [(0, 4, 1), (4, 8, 2)]
                            for (c0, c1, g3) in grps:
                                ei += 1
                                dst = BIG[:, kt * NT + r0 + c0:kt * NT + r0 + c1]
                                srcp = PS[b][:, j * 128 + c0:j * 128 + c1]
                                if b % 2 == 0:
                                    t.op("act", lambda E, dst=dst, srcp=srcp, kt=kt, g3=g3: E.activation(
                                        out=dst, in_=srcp, func=AF.Identity, scale=ABap(l, 0, kt, g3), bias=ABap(l, 1, kt, g3)),
                                        r=[("PS", b), "AB"], pw=["BIG"])
                                else:
                                    t.op("dve", lambda E, dst=dst, srcp=srcp, kt=kt, g3=g3: E.tensor_scalar(
                                        out=dst, in0=srcp, scalar1=ABap(l, 0, kt, g3), scalar2=ABap(l, 1, kt, g3), op0=ALU.mult, op1=ALU.add),
                                        r=[("PS", b), "AB"], pw=["BIG"])

        def proj(l, wsrc, ng, kind_of, BIG):
            with contextlib.ExitStack() as sc:
                Wb = [sb(sc, [128, 8192], BF16) for _ in range(2)]
                stF = [sb(sc, [128, 1032], F32) for _ in range(2)]
                stH = [sb(sc, [128, 1032], BF16) for _ in range(2)]
                sT = [sb(sc, [128, 512], F32) for _ in range(2)]
                sTb = [sb(sc, [128, 512], BF16) for _ in range(2)]
                setbanks = [(0, 1, 2), (3, 4, 5)]
                st = {"si": 0, "tb": 0}

                def epi_pe(info):
                    kind, cbi, h2, s = info
                    if kind not in ("k", "v", "y"):
                        return
                    tok0 = h2 * 1024
                    tiles = [(j, 128) for j in range(8)] + ([(8, NS)] if h2 == 1 else [])
                    for jb in range(0, len(tiles), 4):
                        batch = tiles[jb:jb + 4]
                        tbk = st["tb"] % 2
                        st["tb"] += 1
                        bank = 6 + tbk

                        def tr(E, batch=batch, s=s, bank=bank):
                            ins = None
                            for jj, (j, M) in enumerate(batch):
                                ins = E.transpose(out=PS[bank][0:M, jj * 128:(jj + 1) * 128], in_=stF[s][:, j * 128:j * 128 + M], identity=ident[:, :])
                            return ins
                        t.op("pe", tr, r=[("stF", s), "ident"], w=[("PS", bank)])
                        M = batch[0][1]
                        nj = len(batch)
                        t.op("dve", lambda E, tbk=tbk, bank=bank, M=M, nj=nj: E.tensor_copy(out=sT[tbk][0:M, 0:nj * 128], in_=PS[bank][0:M, 0:nj * 128]),
                             r=[("PS", bank)], w=[("sT", tbk)])
                        r0 = tok0 + batch[0][0] * 128
                        nrow = (nj - 1) * 128 + M
                        if kind == "y":
                            dst = y_scr[r0:r0 + nrow, cbi * 128:(cbi + 1) * 128]
                        elif kind == "k":
                            dst = kp[l, r0:r0 + nrow, (cbi - 16) * 128:(cbi - 15) * 128]
                        else:
                            dst = vp[l, r0:r0 + nrow, (cbi - 32) * 128:(cbi - 31) * 128]
                        if M == 128:
                            dst = dst.rearrange("(j p) c -> p j c", p=128)
                            srcv = sT[tbk][:, 0:nj * 128].rearrange("p (j c) -> p j c", c=128)
                        else:
                            srcv = sT[tbk][0:M, 0:128]
                        t.dma("sp", dst, srcv, r=[("sT", tbk)], pw=["outkvy"], key=("sT", tbk))
                        if kind == "v":
                            t.op("pool", lambda E, tbk=tbk, M=M, nj=nj: E.tensor_copy(out=sTb[tbk][0:M, 0:nj * 128], in_=sT[tbk][0:M, 0:nj * 128]),
                                 r=[("sT", tbk)], w=[("sTb", tbk)])
                            dstb = vb_scr[r0:r0 + nrow, (cbi - 32) * 128:(cbi - 31) * 128]
                            if M == 128:
                                dstb = dstb.rearrange("(j p) c -> p j c", p=128)
                                srcb = sTb[tbk][:, 0:nj * 128].rearrange("p (j c) -> p j c", c=128)
                            else:
                                srcb = sTb[tbk][0:M, 0:128]
                            t.dma("sp", dstb, srcb, r=[("sTb", tbk)], pw=["vb_scr"], key=("sTb", tbk))

                prev = None
                for g in range(ng):
                    ws = g % 2
                    t.dma("pool", Wb[ws][:], wsrc[g], w=[("W", ws)], max_dma_last_dim=8192)
                    for cb in range(2):
                        cbi = g * 2 + cb
                        kind = kind_of(cbi)
                        for h2 in range(2):
                            si = st["si"]
                            st["si"] += 1
                            s = si % 2
                            bs = setbanks[s]
                            tok0 = h2 * 1024
                            chunks = [(0, 512), (512, 512)] + ([(1024, NS)] if h2 == 1 else [])

                            def mm(E, ws=ws, cb=cb, tok0=tok0, chunks=chunks, bs=bs):
                                ins = None
                                for kt in range(32):
                                    for j, (c0, n) in enumerate(chunks):
                                        ins = E.matmul(PS[bs[j]][:, 0:n], lhsT=Wb[ws][:, kt * 256 + cb * 128:kt * 256 + (cb + 1) * 128],
                                                       rhs=BIG[:, kt * NT + tok0 + c0:kt * NT + tok0 + c0 + n], start=(kt == 0), stop=(kt == 31))
                                return ins
                            t.op("pe", mm, r=[("W", ws), "BIG"], w=[("PS", bs[j]) for j in range(len(chunks))])
                            tgt, tkey = (stH[s], ("stH", s)) if kind == "q" else (stF[s], ("stF", s))
                            fn = AF.Silu if kind in ("ga", "gl") else AF.Copy
                            for j, (c0, n) in enumerate(chunks):
                                t.op("act", lambda E, tgt=tgt, c0=c0, n=n, bj=bs[j], fn=fn: E.activation(out=tgt[:, c0:c0 + n], in_=PS[bj][:, 0:n], func=fn),
                                     r=[("PS", bs[j])], w=[tkey] if j == 0 else [], pw=[] if j == 0 else [tkey])
                            ntok = chunks[-1][0] + chunks[-1][1]
                            hh = cbi % 16
                            if kind == "q":
                                t.dma("sp", qT_scr[hh, :, tok0:tok0 + ntok], stH[s][:, 0:ntok], r=[tkey], pw=["qT_scr"], key=tkey)
                            elif kind == "k":
                                t.op("dve", lambda E, s=s, ntok=ntok: E.tensor_copy(out=stH[s][:, 0:ntok], in_=stF[s][:, 0:ntok]),
                                     r=[("stF", s)], w=[("stH", s)])
                                t.dma("sp", kT_scr[hh, :, tok0:tok0 + ntok], stH[s][:, 0:ntok], r=[("stH", s)], pw=["kT_scr"], key=("stH", s))
                            elif kind in ("ga", "xl", "gl"):
                                dsc = {"ga": sg_scr, "xl": xl_scr, "gl": sgl_scr}[kind]
                                t.dma("sp", dsc[hh, :, tok0:tok0 + ntok], stF[s][:, 0:ntok], r=[tkey], pw=["scr_" + kind], key=tkey)
                            if prev is not None:
                                epi_pe(prev)
                            prev = (kind, cbi, h2, s)
                epi_pe(prev)

        def kind_in(cbi):
            return ["q", "k", "v", "ga", "xl", "gl"][cbi // 16]

        def phase_c(l, do_ada_next):
            with contextlib.ExitStack() as sc:
                def g_attn():
                    qT = [sb(sc, [128, NP], BF16) for _ in range(2)]
                    kT = [sb(sc, [128, NP], BF16) for _ in range(2)]
                    V = [sb(sc, [128, 2048], BF16) for _ in range(2)]
                    wbm = [sb(sc, [128, TW], BF16) for _ in range(2)]
                    sgc = [sb(sc, [128, 512], F32) for _ in range(2)]
                    Eb = [sb(sc, [128, 512], BF16) for _ in range(2)]
                    Pb = [sb(sc, [128, 512], BF16) for _ in range(2)]
                    lnd = [sb(sc, [128, 512], F32) for _ in range(2)]
                    tmpo = [sb(sc, [128, 512], F32) for _ in range(2)]
                    uo = [sb(sc, [128, 512], BF16) for _ in range(2)]
                    qi = 0
                    ii = 0
                    for h in range(H):
                        hs = h % 2
                        hc = slice(h * 128, (h + 1) * 128)
                        t.dma("sp", qT[hs][:], qT_scr[h, :, 0:NP], w=[("qT", hs)])
                        t.dma("sp", kT[hs][:], kT_scr[h, :, 0:NP], w=[("kT", hs)])
                        t.dma("sp", V[hs][:].rearrange("p (j c) -> p j c", c=128), vb_scr[0:NP, hc].rearrange("(j p) c -> p j c", p=128), w=[("V", hs)])
                        t.dma("sp", wbm[hs][:], wb_scr[h], w=[("wbm", hs)])
                        for qc in range(4):
                            qs = qi % 2
                            qi += 1
                            t.dma("sp", sgc[qs][:], sg_scr[h, :, qc * 512:(qc + 1) * 512], w=[("sgc", qs)])
                            nkb = 4 * qc + 4
                            po, pd = 2, 3

                            def pv(i, kb, iis, nkb=nkb, hs=hs):
                                first, last = (i == 0), (i == nkb - 1)
                                t.op("pe", lambda E: E.matmul(PS[po][:, :], lhsT=V[hs][:, kb * 128:(kb + 1) * 128], rhs=Pb[iis][:, :], start=first, stop=last),
                                     r=[("V", hs), ("Pb", iis)], w=[("PS", po)] if first else [], pw=[] if first else [("PS", po)])
                                t.op("pe", lambda E: E.matmul(PS[pd][:, :], lhsT=ones[:, :], rhs=Pb[iis][:, :], start=first, stop=last),
                                     r=["ones", ("Pb", iis)], w=[("PS", pd)] if first else [], pw=[] if first else [("PS", pd)])
                            pend = None
                            for i in range(nkb):
                                kb = i
                                iis = ii % 2
                                ii += 1
                                off = 384 + qc * 512 - kb * 128
                                t.op("pe", lambda E, iis=iis, kb=kb: E.matmul(PS[iis][:, :], lhsT=kT[hs][:, kb * 128:(kb + 1) * 128],
                                                                              rhs=qT[hs][:, qc * 512:(qc + 1) * 512], start=True, stop=True),
                                     r=[("kT", hs), ("qT", hs)], w=[("PS", iis)])
                                t.op("act", lambda E, iis=iis: E.activation(out=Eb[iis][:, :], in_=PS[iis][:, :], func=AF.Exp, scale=SCALE),
                                     r=[("PS", iis)], w=[("Eb", iis)])
                                t.op("dve", lambda E, iis=iis, off=off: E.tensor_tensor(out=Pb[iis][:, :], in0=Eb[iis][:, :], in1=wbm[hs][:, off:off + 512], op=ALU.mult),
                                     r=[("Eb", iis), ("wbm", hs)], w=[("Pb", iis)])
                                if pend is not None:
                                    pv(*pend)
                                pend = (i, kb, iis)
                                yield
                            pv(*pend)
                            t.op("act", lambda E, qs=qs: E.activation(out=lnd[qs][:, :], in_=PS[pd][:, :], func=AF.Ln), r=[("PS", pd)], w=[("lnd", qs)])
                            t.op("act", lambda E, qs=qs: E.activation(out=lnd[qs][:, :], in_=lnd[qs][:, :], func=AF.Exp, scale=-1.0), w=[("lnd", qs)])
                            t.op("dve", lambda E, qs=qs: E.tensor_tensor(out=tmpo[qs][:, :], in0=PS[po][:, :], in1=lnd[qs][:, :], op=ALU.mult),
                                 r=[("PS", po), ("lnd", qs)], w=[("tmpo", qs)])
                            t.op("pool", lambda E, qs=qs: E.tensor_tensor(out=uo[qs][:, :], in0=tmpo[qs][:, :], in1=sgc[qs][:, :], op=ALU.mult),
                                 r=[("tmpo", qs), ("sgc", qs)], w=[("uo", qs)])
                            t.dma("pool", u_scr[h, :, qc * 512:(qc + 1) * 512], uo[qs][:, :], r=[("uo", qs)], pw=["u_scr"], key=("uo", qs))
                            yield

                US = sb(sc, [128, 32 * NS], BF16)

                def g_samp():
                    qs_ = [sb(sc, [128, NS], BF16) for _ in range(2)]
                    kn = [sb(sc, [128, NS], BF16) for _ in range(2)]
                    vn = [[sb(sc, [4, 128], BF16) for _ in range(2)] for _ in range(2)]
                    wbs = [sb(sc, [128, TW], BF16) for _ in range(2)]
                    sgs = [sb(sc, [128, NS], F32) for _ in range(2)]
                    Kc = [sb(sc, [128, 2048], F32) for _ in range(2)]
                    KcT = [sb(sc, [128, 2048], BF16) for _ in range(2)]
                    Vc = [sb(sc, [128, 2048], BF16) for _ in range(2)]
                    Es = [sb(sc, [128, 68], F32) for _ in range(2)]
                    Ps = [sb(sc, [128, 68], BF16) for _ in range(2)]
                    rcs = [sb(sc, [128, 4], F32) for _ in range(2)]
                    tms = [sb(sc, [128, 4], F32) for _ in range(2)]
                    ci = 0
                    for h in range(H):
                        hs = h % 2
                        hc = slice(h * 128, (h + 1) * 128)
                        t.dma("sp", qs_[hs][:], qT_scr[h, :, NP:NT], w=[("qs", hs)])
                        t.dma("sp", kn[hs][:], kT_scr[h, :, NP:NT], w=[("kn", hs)])
                        for db in range(2):
                            t.dma("sp", vn[hs][db][:], vb_scr[NP + db * 4:NP + db * 4 + 4, hc], w=[("vn", hs, db)])
                        t.dma("sp", wbs[hs][:], wb_scr[h], w=[("wbs", hs)])
                        t.dma("sp", sgs[hs][:], sg_scr[h, :, NP:NT], w=[("sgs", hs)])
                        for db in range(2):
                            c = ci % 2
                            ci += 1
                            t.dma("sp", Kc[c][:].rearrange("p (j c) -> p j c", c=128), ck[l, db, :, hc].rearrange("(j p) c -> p j c", p=128), w=[("Kc", c)])
                            t.dma("pool", Vc[c][:].rearrange("p (j c) -> p j c", c=128), cv[l, db, :, hc].rearrange("(j p) c -> p j c", p=128), w=[("Vc", c)])
                            yield
                            for jb in range(4):
                                def tr(E, jb=jb, c=c):
                                    ins = None
                                    for jj in range(4):
                                        blk = jb * 4 + jj
                                        ins = E.transpose(out=PS[5][:, jj * 128:(jj + 1) * 128], in_=Kc[c][:, blk * 128:(blk + 1) * 128], identity=ident[:, :])
                                    return ins
                                t.op("pe", tr, r=[("Kc", c), "ident"], w=[("PS", 5)])
                                t.op("dve", lambda E, jb=jb, c=c: E.tensor_copy(out=KcT[c][:, jb * 512:(jb + 1) * 512], in_=PS[5][:, :]),
                                     r=[("PS", 5)], w=[("KcT", c)] if jb == 0 else [], aw=[] if jb == 0 else [("KcT", c)])
                                yield
                            qcol = slice(db * 4, db * 4 + 4)

                            def smm(E, c=c, qcol=qcol):
                                ins = None
                                for j in range(16):
                                    blk = 15 - j
                                    ins = E.matmul(PS[4][:, j * 4:(j + 1) * 4], lhsT=KcT[c][:, blk * 128:(blk + 1) * 128], rhs=qs_[hs][:, qcol], start=True, stop=True,
                                                   skip_group_check=True)
                                ins = E.matmul(PS[4][0:4, 64:68], lhsT=kn[hs][:, qcol], rhs=qs_[hs][:, qcol], start=True, stop=True, skip_group_check=True)
                                return ins
                            t.op("pe", smm, r=[("KcT", c), ("kn", hs), ("qs", hs)], w=[("PS", 4)])
                            t.op("act", lambda E, c=c: E.activation(out=Es[c][:, 0:64], in_=PS[4][:, 0:64], func=AF.Exp, scale=SCALE), r=[("PS", 4)], w=[("Es", c)])
                            t.op("act", lambda E, c=c: E.activation(out=Es[c][0:4, 64:68], in_=PS[4][0:4, 64:68], func=AF.Exp, scale=SCALE), r=[("PS", 4)], aw=[("Es", c)])
                            t.op("dve", lambda E, c=c: E.tensor_tensor(out=Ps[c][:, 0:64].rearrange("p (j c) -> p j c", c=4), in0=Es[c][:, 0:64].rearrange("p (j c) -> p j c", c=4),
                                                                        in1=wbs[hs][:, 512:2560].rearrange("p (j c) -> p j c", c=128)[:, :, 0:4], op=ALU.mult),
                                 r=[("Es", c), ("wbs", hs)], w=[("Ps", c)])
                            t.op("dve", lambda E, c=c: E.tensor_tensor(out=Ps[c][0:4, 64:68], in0=Es[c][0:4, 64:68], in1=wbs[hs][0:4, 384:388], op=ALU.mult),
                                 r=[("Es", c), ("wbs", hs)], aw=[("Ps", c)])
                            yield

                            def spv(E, c=c, db=db):
                                ins = None
                                for j in range(16):
                                    blk = 15 - j
                                    E.matmul(PS[4][:, 128:132], lhsT=Vc[c][:, blk * 128:(blk + 1) * 128], rhs=Ps[c][:, j * 4:(j + 1) * 4], start=(j == 0), stop=False,
                                             skip_group_check=True)
                                E.matmul(PS[4][:, 128:132], lhsT=vn[hs][db][:, :], rhs=Ps[c][0:4, 64:68], start=False, stop=True, skip_group_check=True)
                                for j in range(16):
                                    E.matmul(PS[4][:, 136:140], lhsT=ones[:, :], rhs=Ps[c][:, j * 4:(j + 1) * 4], start=(j == 0), stop=False, skip_group_check=True)
                                ins = E.matmul(PS[4][:, 136:140], lhsT=ones[0:4, :], rhs=Ps[c][0:4, 64:68], start=False, stop=True, skip_group_check=True)
                                return ins
                            t.op("pe", spv, r=[("Ps", c), ("Vc", c), ("vn", hs, db), "ones"], w=[("PS", 4)])
                            t.op("dve", lambda E, c=c: E.reciprocal(out=rcs[c][:, :], in_=PS[4][:, 136:140]), r=[("PS", 4)], w=[("rcs", c)])
                            t.op("dve", lambda E, c=c: E.tensor_tensor(out=tms[c][:, :], in0=PS[4][:, 128:132], in1=rcs[c][:, :], op=ALU.mult), r=[("PS", 4), ("rcs", c)], w=[("tms", c)])
                            t.op("pool", lambda E, c=c, db=db: E.tensor_tensor(out=US[:, h * NS + db * 4:h * NS + db * 4 + 4], in0=tms[c][:, :],
                                                                                in1=sgs[hs][:, db * 4:db * 4 + 4], op=ALU.mult),
                                 r=[("tms", c), ("sgs", hs)], pw=["US"])
                            yield

                def g_lru():
                    wc = sb(sc, [128, NB * 4], F32)
                    prm = sb(sc, [128, 4 * NB], F32)
                    c1 = sb(sc, [128, NB], F32)
                    c2 = sb(sc, [128, NB], F32)
                    etmp = sb(sc, [128, NB], F32)
                    wab = sb(sc, [128, NB * 128], BF16)
                    wxb = sb(sc, [128, NB * 128], BF16)
                    t.dma("sp", wc[:], wconvT[l].rearrange("p n j -> p (n j)"), w=["wc"])
                    t.dma("sp", prm[:], lrup[l].rearrange("i p n -> p i n"), w=["prm"])
                    t.dma("pool", wab[:].rearrange("p (n e) -> p n e", e=128), wa_t[l].rearrange("n d e -> d n e"), w=["wab"])
                    t.dma("pool", wxb[:].rearrange("p (n e) -> p n e", e=128), wx_t[l].rearrange("n d e -> d n e"), w=["wxb"])
                    t.op("act", lambda E: E.activation(out=etmp[:], in_=prm[:, 3 * NB:4 * NB], func=AF.Exp, scale=-1.0), r=["prm"], w=["etmp"])
                    t.op("act", lambda E: E.activation(out=etmp[:], in_=etmp[:], func=AF.Ln, bias=1.0), w=["etmp"])
                    t.op("dve", lambda E: E.tensor_scalar(out=c1[:], in0=etmp[:], scalar1=-8.0, scalar2=None, op0=ALU.mult), r=["etmp"], w=["c1"])
                    t.op("dve", lambda E: E.tensor_scalar(out=c2[:], in0=etmp[:], scalar1=-16.0, scalar2=None, op0=ALU.mult), r=["etmp"], w=["c2"])
                    nprm = sb(sc, [128, 4 * NB], F32)
                    t.op("dve", lambda E: E.tensor_scalar(out=nprm[:], in0=prm[:], scalar1=-1.0, scalar2=None, op0=ALU.mult), r=["prm"], w=["nprm"])
                    NL = 1024
                    xe = [sb(sc, [128, NL + 3], F32) for _ in range(2)]
                    sgl = [sb(sc, [128, NL], F32) for _ in range(2)]
                    xc = sb(sc, [128, NL], F32)
                    xcb = sb(sc, [128, NL], BF16)
                    rr = sb(sc, [128, NL], F32)
                    ig = sb(sc, [128, NL], F32)
                    aa = sb(sc, [128, NL], F32)
                    sq = sb(sc, [128, NL], F32)
                    hh = [sb(sc, [128, NL], F32) for _ in range(2)]
                    ul = [sb(sc, [128, NL], BF16) for _ in range(2)]
                    h0 = sb(sc, [128, 2], F32)
                    ci = 0
                    for n in range(NB):
                        items = [("p", c, NL) for c in range(2)] + [("s", db, 4) for db in range(2)]
                        prev_h = None
                        for (knd, idx, N) in items:
                            s = ci % 2
                            ci += 1
                            if knd == "p":
                                c0 = idx * NL
                                if idx == 0:
                                    t.op("dve", lambda E, s=s: E.memset(xe[s][:, 0:3], 0.0), w=[("xe", s)])
                                    t.dma("sp", xe[s][:, 3:NL + 3], xl_scr[n, :, 0:NL], aw=[("xe", s)], key=("xe", s))
                                else:
                                    t.dma("sp", xe[s][:, 0:NL + 3], xl_scr[n, :, c0 - 3:c0 + NL], w=[("xe", s)])
                                t.dma("sp", sgl[s][:, 0:N], sgl_scr[n, :, c0:c0 + N], w=[("sgl", s)])
                                init = 0.0 if idx == 0 else prev_h
                            else:
                                db = idx
                                t.dma("sp", xe[s][:, 0:3], sconvT[l, db, n * 128:(n + 1) * 128, :], w=[("xe", s)])
                                t.dma("sp", xe[s][:, 3:7], xl_scr[n, :, NP + db * 4:NP + db * 4 + 4], aw=[("xe", s)], key=("xe", s))
                                t.dma("sp", sgl[s][:, 0:N], sgl_scr[n, :, NP + db * 4:NP + db * 4 + 4], w=[("sgl", s)])
                                t.dma("sp", h0[:, db:db + 1], sh[l, db, n * 128:(n + 1) * 128].rearrange("(p o) -> p o", o=1), w=[("h0", db)])
                                init = h0[:, db:db + 1]
                            w4 = [wc[:, n * 4 + j:n * 4 + j + 1] for j in range(4)]
                            t.op("dve", lambda E, s=s, N=N, w4=w4: E.tensor_scalar(out=xc[:, 0:N], in0=xe[s][:, 3:3 + N], scalar1=w4[3], scalar2=prm[:, n:n + 1],
                                                                                   op0=ALU.mult, op1=ALU.add), r=[("xe", s), "wc", "prm"], w=["xc"])
                            for j in range(3):
                                t.op("dve", lambda E, s=s, N=N, j=j, w4=w4: E.scalar_tensor_tensor(out=xc[:, 0:N], in0=xe[s][:, j:j + N], scalar=w4[j], in1=xc[:, 0:N],
                                                                                                    op0=ALU.mult, op1=ALU.add), r=[("xe", s)], w=["xc"])
                            yield
                            t.op("pool", lambda E, N=N: E.tensor_copy(out=xcb[:, 0:N], in_=xc[:, 0:N]), r=["xc"], w=["xcb"])
                            yield
                            chunks = [(c0_, min(512, N - c0_)) for c0_ in range(0, N, 512)]
                            k2 = 0
                            for (wmat, dst, bofs, key) in ((wab, rr, NB, "rr"), (wxb, ig, 2 * NB, "ig")):
                                for (cc0, cn) in chunks:
                                    pbk = 6 + (k2 % 2)
                                    k2 += 1
                                    t.op("pe", lambda E, wmat=wmat, cc0=cc0, cn=cn, pbk=pbk: E.matmul(PS[pbk][:, 0:cn], lhsT=wmat[:, n * 128:(n + 1) * 128],
                                                                                                      rhs=xcb[:, cc0:cc0 + cn], start=True, stop=True),
                                         r=["wab", "wxb", "xcb"], w=[("PS", pbk)])
                                    fst = (cc0 == 0)
                                    t.op("act", lambda E, dst=dst, cc0=cc0, cn=cn, pbk=pbk, bofs=bofs: E.activation(
                                        out=dst[:, cc0:cc0 + cn], in_=PS[pbk][:, 0:cn], func=AF.Exp, scale=-1.0, bias=nprm[:, bofs + n:bofs + n + 1]),
                                        r=[("PS", pbk), "nprm"], w=[key] if fst else [], aw=[] if fst else [key])
                                    yield
                                t.op("act", lambda E, dst=dst, N=N: E.activation(out=dst[:, 0:N], in_=dst[:, 0:N], func=AF.Ln, bias=1.0), w=[key])
                                t.op("act", lambda E, dst=dst, N=N: E.activation(out=dst[:, 0:N], in_=dst[:, 0:N], func=AF.Exp, scale=-1.0), w=[key])
                                yield
                            t.op("act", lambda E, N=N: E.activation(out=aa[:, 0:N], in_=rr[:, 0:N], func=AF.Exp, scale=c1[:, n:n + 1]), r=["rr", "c1"], w=["aa"])
                            t.op("act", lambda E, N=N: E.activation(out=sq[:, 0:N], in_=rr[:, 0:N], func=AF.Exp, scale=c2[:, n:n + 1]), r=["rr", "c2"], w=["sq"])
                            t.op("act", lambda E, N=N: E.activation(out=sq[:, 0:N], in_=sq[:, 0:N], func=AF.Ln, scale=-1.0, bias=1.0), w=["sq"])
                            t.op("act", lambda E, N=N: E.activation(out=sq[:, 0:N], in_=sq[:, 0:N], func=AF.Exp, scale=0.5), w=["sq"])
                            yield
                            t.op("pool", lambda E, N=N: E.tensor_tensor(out=ig[:, 0:N], in0=ig[:, 0:N], in1=xc[:, 0:N], op=ALU.mult), r=["xc"], w=["ig"])
                            t.op("pool", lambda E, N=N: E.tensor_tensor(out=ig[:, 0:N], in0=ig[:, 0:N], in1=sq[:, 0:N], op=ALU.mult), r=["sq"], w=["ig"])
                            yield
                            rds = ["aa", "ig"]
                            if knd == "p" and idx > 0:
                                rds.append(("hh", 1 - s))
                            if knd == "s":
                                rds.append(("h0", idx))
                            t.op("dve", lambda E, s=s, N=N, init=init: E.tensor_tensor_scan(out=hh[s][:, 0:N], data0=aa[:, 0:N], data1=ig[:, 0:N], initial=init,
                                                                                             op0=ALU.mult, op1=ALU.add), r=rds, w=[("hh", s)])
                            prev_h = hh[s][:, N - 1:N]
                            yield
                            if knd == "p":
                                t.op("pool", lambda E, s=s, N=N: E.tensor_tensor(out=ul[s][:, 0:N], in0=hh[s][:, 0:N], in1=sgl[s][:, 0:N], op=ALU.mult),
                                     r=[("hh", s), ("sgl", s)], w=[("ul", s)])
                                t.dma("pool", u_scr[16 + n, :, c0:c0 + N], ul[s][:, 0:N], r=[("ul", s)], pw=["u_scr"], key=("ul", s))
                                if idx == 1:
                                    t.dma("pool", hp[l, 0, n * 128:(n + 1) * 128].rearrange("(p o) -> p o", o=1), hh[s][:, NL - 1:NL], r=[("hh", s)], pw=["hp"], key=("hho", s))
                                    t.dma("pool", convT[l, 0, n * 128:(n + 1) * 128, :], xe[s][:, NL:NL + 3], r=[("xe", s)], pw=["convT"], key=("xeo", s))
                            else:
                                t.op("pool", lambda E, s=s, N=N, idx=idx: E.tensor_tensor(out=US[:, (16 + n) * NS + idx * 4:(16 + n) * NS + idx * 4 + 4], in0=hh[s][:, 0:N],
                                                                                          in1=sgl[s][:, 0:N], op=ALU.mult),
                                     r=[("hh", s), ("sgl", s)], pw=["US"])
                                t.dma("pool", hp[l, 1 + idx, n * 128:(n + 1) * 128].rearrange("(p o) -> p o", o=1), hh[s][:, 3:4], r=[("hh", s)], pw=["hp"], key=("hho", s))
                                t.dma("pool", convT[l, 1 + idx, n * 128:(n + 1) * 128, :], xe[s][:, 4:7], r=[("xe", s)], pw=["convT"], key=("xeo", s))
                            yield

                gens = [g_attn(), g_samp(), g_lru()]
                if do_ada_next:
                    gens.append(ada_gen(l + 1, ada_tiles(sc), 5, list(range(NGI))))
                run_gens(gens)
                t.dma("pool", u_scr[:, :, NP:NT].rearrange("k p t -> p k t"), US[:].rearrange("p (k t) -> p k t", t=NS), r=["US"], pw=["u_scr"], key="USo")
                t.barrier()

        def phase_e(l, xsrc, xdst):
            with contextlib.ExitStack() as sc:
                GGp = sb(sc, [128, D], F32)
                GGs = sb(sc, [NS, D], F32)
                gpb = sb(sc, [128, D], F32)
                yt = [sb(sc, [128, D], F32) for _ in range(2)]
                xt = [sb(sc, [128, D], F32) for _ in range(2)]
                junk = sb(sc, [128, D], BF16)
                stat = sb(sc, [128, 17 * 4], F32)
                t.dma("sp", GGp[:], bass.AP(mod_h, (l * 3 + 0) * D, [[0, 128], [1, D]]), w=["GGp"])
                t.dma("sp", GGs[0:4, :], bass.AP(mod_h, (l * 3 + 1) * D, [[0, 4], [1, D]]), w=["GGs"])
                t.dma("sp", GGs[4:8, :], bass.AP(mod_h, (l * 3 + 2) * D, [[0, 4], [1, D]]), aw=["GGs"], key="GGs")
                t.dma("sp", gpb[:], bass.AP(gpost_h, l * D, [[0, 128], [1, D]]), w=["gpb"])
                t.op("dve", lambda E: E.tensor_tensor(out=GGp[:], in0=GGp[:], in1=gpb[:], op=ALU.mult), r=["gpb"], w=["GGp"])
                t.op("dve", lambda E: E.tensor_tensor(out=GGs[:], in0=GGs[:], in1=gpb[0:NS, :], op=ALU.mult), r=["gpb"], w=["GGs"])
                for tt in range(17):
                    M = 128 if tt < 16 else NS
                    r0 = tt * 128
                    s = tt % 2
                    so = tt * 4
                    GG = GGp if tt < 16 else GGs
                    gk = "GGp" if tt < 16 else "GGs"
                    t.dma("sp", yt[s][0:M, :], y_scr[r0:r0 + M, :], w=[("yt", s)])
                    t.dma("sp", xt[s][0:M, :], xsrc[r0:r0 + M, :], w=[("xt", s)])
                    t.op("act", lambda E, s=s, M=M, so=so: E.activation(out=junk[0:M, :], in_=yt[s][0:M, :], func=AF.Square, accum_out=stat[0:M, so:so + 1]),
                         r=[("yt", s)], w=["junk", ("st", 0)])
                    t.op("dve", lambda E, M=M, so=so: E.tensor_scalar(out=stat[0:M, so + 1:so + 2], in0=stat[0:M, so:so + 1], scalar1=1.0 / D, scalar2=EPS,
                                                                       op0=ALU.mult, op1=ALU.add), r=[("st", 0)], w=[("st", 1)])
                    t.op("act", lambda E, M=M, so=so: E.activation(out=stat[0:M, so + 2:so + 3], in_=stat[0:M, so + 1:so + 2], func=AF.Sqrt),
                         r=[("st", 1)], w=[("st", 2)])
                    t.op("dve", lambda E, M=M, so=so: E.reciprocal(out=stat[0:M, so + 3:so + 4], in_=stat[0:M, so + 2:so + 3]), r=[("st", 2)], w=[("st", 3)])
                    t.op("dve", lambda E, s=s, M=M, so=so, GG=GG: E.scalar_tensor_tensor(out=yt[s][0:M, :], in0=yt[s][0:M, :], scalar=stat[0:M, so + 3:so + 4],
                                                                                         in1=GG[0:M, :], op0=ALU.mult, op1=ALU.mult),
                         r=[("st", 3), gk], w=[("yt", s)])
                    t.op("pool", lambda E, s=s, M=M: E.tensor_tensor(out=xt[s][0:M, :], in0=xt[s][0:M, :], in1=yt[s][0:M, :], op=ALU.add),
                         r=[("yt", s)], w=[("xt", s)])
                    t.dma("pool", xdst[r0:r0 + M, :], xt[s][0:M, :], r=[("xt", s)], pw=["xdst"], key=("xto", s))

        for l in range(DEPTH):
            xsrc = x_in if l == 0 else x1_scr
            xdst = x1_scr if l == 0 else yp
            with contextlib.ExitStack() as sc:
                BIG = sb(sc, [128, 32 * NT], BF16, "BIGh")
                with nc.named_scope(f"A{l}"):
                    phase_a(l, xsrc, BIG)
                    t.barrier()
                if stop <= 2:
                    return nc
                with nc.named_scope(f"B{l}"):
                    proj(l, win[l], NGI, kind_in, BIG)
                    t.barrier()
            if stop <= 3:
                return nc
            with nc.named_scope(f"C{l}"):
                phase_c(l, l + 1 < DEPTH)
            if stop <= 4:
                return nc
            with contextlib.ExitStack() as sc:
                BIG = sb(sc, [128, 32 * NT], BF16, "BIGu")
                with nc.named_scope(f"D{l}"):
                    t.dma("sp", BIG[:].rearrange("p (k t) -> p k t", t=NT), u_scr[:, :, 0:NT].rearrange("k p t -> p k t"), w=["BIG"])
                    proj(l, wout[l], NGO, lambda cbi: "y", BIG)
                    t.barrier()
            if stop <= 5:
                return nc
            with nc.named_scope(f"E{l}"):
                phase_e(l, xsrc, xdst)
                t.barrier()
            if stop <= 6:
                return nc
    return nc


def _bucket(i):
    if i < 16:
        return i
    d = np.float32(i)
    v = np.float32(16) + np.log(np.maximum(d, np.float32(1.0)) / np.float32(16)) / np.float32(math.log(2048 / 16)) * np.float32(16)
    return int(min(int(np.float32(v)), 31))


def _consts():
    cp = np.zeros((32, TVL), np.float32)
    for i in range(0, 2049):
        b = _bucket(i)
        cnt = 0
        for (wdw, dil) in ((128, 1), (512, 4), (2048, 16)):
            if i % dil == 0 and i // dil <= wdw // dil:
                cnt += 1
        if cnt:
            cp[b, i + TOFF] += cnt
    jm = np.eye(128, dtype=np.float32)[::-1].copy()
    return cp, jm, np.eye(128, dtype=np.float32)


_NC_CACHE = {}
_NCORES = [8]
_STOP = [99]
_RUNKW = {}
_DBG = {'ntt': 17, 'evac': True}


def kernel(x_prompt, x_sample, cache_k, cache_v, state_h, state_conv, c_prompt, c_sample, rel_table, w_ada, b_ada,
           g_pre, w_in, w_conv, b_conv, w_a, b_a, w_x, b_x, lam, w_out, g_post):
    f = lambda a: np.ascontiguousarray(np.asarray(a, dtype=np.float32))
    x_prompt, x_sample, cache_k, cache_v = f(x_prompt), f(x_sample), f(cache_k), f(cache_v)
    state_h, state_conv, c_prompt, c_sample = f(state_h), f(state_conv), f(c_prompt), f(c_sample)
    cp, jm, idn = _consts()

    def tile_w(w, ng):
        L = w.shape[0]
        return np.ascontiguousarray(f(w).reshape(L, 32, 128, ng, 256).transpose(0, 3, 2, 1, 4)).reshape(L, ng, 128, 8192)
    shared = {
        "rel": f(rel_table), "cpad": cp, "jmat": jm, "ident": idn,
        "wada": tile_w(w_ada, NGI), "bada": f(b_ada),
        "gpreT": np.ascontiguousarray(f(g_pre).reshape(DEPTH, 32, 128).transpose(0, 2, 1)),
        "win": tile_w(w_in, NGI), "wout": tile_w(w_out, NGO),
        "wconvT": np.ascontiguousarray(f(w_conv).reshape(DEPTH, 4, NB, 128).transpose(0, 3, 2, 1)),
        "lrup": np.ascontiguousarray(np.stack([f(b_conv), f(b_a), f(b_x), f(lam)], axis=1).reshape(DEPTH, 4, NB, 128).transpose(0, 1, 3, 2)),
        "wa_t": f(w_a), "wx_t": f(w_x), "gpost": f(g_post),
    }
    in_maps = []
    for c in range(_NCORES[0]):
        b = c % 4
        m = dict(shared)
        m["x_in"] = np.ascontiguousarray(np.concatenate([x_prompt[b], x_sample[2 * b], x_sample[2 * b + 1]], axis=0))
        m["c3"] = np.ascontiguousarray(np.stack([c_prompt[b], c_sample[2 * b], c_sample[2 * b + 1]], axis=0))
        m["ck"] = np.ascontiguousarray(cache_k[:, 2 * b:2 * b + 2].reshape(DEPTH, 2, NP, H * 128))
        m["cv"] = np.ascontiguousarray(cache_v[:, 2 * b:2 * b + 2].reshape(DEPTH, 2, NP, H * 128))
        m["sh"] = np.ascontiguousarray(state_h[:, 2 * b:2 * b + 2])
        m["sconvT"] = np.ascontiguousarray(state_conv[:, 2 * b:2 * b + 2].transpose(0, 1, 3, 2))
        in_maps.append(m)
    if "nc" not in _NC_CACHE:
        _NC_CACHE["nc"] = build_nc(_STOP[0])
    res = run_bass_kernel_spmd(_NC_CACHE["nc"], in_maps, core_ids=list(range(_NCORES[0])), **_RUNKW)
    _DBG["res"] = res
    R = res.results
    if _NCORES[0] < 8:
        R = [R[c % _NCORES[0]] for c in range(8)]
    B, DB = 4, 8
    y_p = np.zeros((B, NP, D), np.float32)
    y_s = np.zeros((DB, 4, D), np.float32)
    k_p = np.zeros((DEPTH, B, NP, H, 128), np.float32)
    v_p = np.zeros_like(k_p)
    h_p = np.zeros((DEPTH, B, 2048), np.float32)
    c_p = np.zeros((DEPTH, B, 3, 2048), np.float32)
    k_s = np.zeros((DEPTH, DB, 4, H, 128), np.float32)
    v_s = np.zeros_like(k_s)
    h_s = np.zeros((DEPTH, DB, 2048), np.float32)
    c_s = np.zeros((DEPTH, DB, 3, 2048), np.float32)
    for b in range(B):
        r = R[b]
        y_p[b] = r["yp"][:NP]
        kpr, vpr, hpr, cvr = r["kp"], r["vp"], r["hp"], r["convT"]
        k_p[:, b] = kpr[:, :NP].reshape(DEPTH, NP, H, 128)
        v_p[:, b] = vpr[:, :NP].reshape(DEPTH, NP, H, 128)
        h_p[:, b] = hpr[:, 0]
        c_p[:, b] = cvr[:, 0].transpose(0, 2, 1)
        for db in range(2):
            y_s[2 * b + db] = r["yp"][NP + 4 * db:NP + 4 * db + 4]
            k_s[:, 2 * b + db] = kpr[:, NP + 4 * db:NP + 4 * db + 4].reshape(DEPTH, 4, H, 128)
            v_s[:, 2 * b + db] = vpr[:, NP + 4 * db:NP + 4 * db + 4].reshape(DEPTH, 4, H, 128)
            h_s[:, 2 * b + db] = hpr[:, 1 + db]
            c_s[:, 2 * b + db] = cvr[:, 1 + db].transpose(0, 2, 1)
    return (y_p, y_s, k_p, v_p, h_p, c_p, k_s, v_s, h_s, c_s)
```
